# Optimizing a Trainium2 kernel written in Bass

```python
import jax, jax.numpy as jnp
from jax import lax
import numpy as np

D_MODEL = 2048
BATCH = 2
SEQ = 4096
DEPTH = 2

N_META = 16
EPS = 1e-6
A_WIDTH = D_MODEL // 2
A_CONV = 3
B_WIDTH = D_MODEL // 2
B_WINDOWS = (2, 4, 8, 16)
B_GROUPS = len(B_WINDOWS)
B_GROUP_DIM = B_WIDTH // B_GROUPS
IN0_COLS = 3 * A_WIDTH + B_WIDTH
MIX0_WIDTH = A_WIDTH + B_WIDTH
C_WIDTH = D_MODEL
C_CONV = 31
D_FF = 5632
FFN_CONV = 3
N_EVEN = (DEPTH + 1) // 2
N_ODD = DEPTH // 2

kernel_name = "hybrid_shortconv_pool_conformer_trunk"


def rms_norm(x, g):
    xf = x.astype(jnp.float32)
    y = xf * lax.rsqrt(jnp.mean(xf * xf, axis=-1, keepdims=True) + EPS)
    return (y * g.astype(jnp.float32)).astype(x.dtype)


def layer_norm(x, g, b):
    xf = x.astype(jnp.float32)
    mu = jnp.mean(xf, axis=-1, keepdims=True)
    var = jnp.mean(jnp.square(xf - mu), axis=-1, keepdims=True)
    y = (xf - mu) * lax.rsqrt(var + EPS)
    return (y * g.astype(jnp.float32) + b.astype(jnp.float32)).astype(x.dtype)


def causal_dwconv(x, w):
    K, C = w.shape
    return lax.conv_general_dilated(
        x, w[:, None, :].astype(x.dtype), window_strides=(1,),
        padding=[(K - 1, 0)], dimension_numbers=('NWC', 'WIO', 'NWC'),
        feature_group_count=C)


def causal_multiscale_pool(v):
    S = v.shape[1]
    vf = v.astype(jnp.float32)
    cs = jnp.cumsum(vf, axis=1)
    t1 = jnp.arange(1, S + 1, dtype=jnp.float32)
    outs = []
    for g, w in enumerate(B_WINDOWS):
        cs_g = cs[:, :, g]
        prev = jnp.pad(cs_g, ((0, 0), (w, 0), (0, 0)))[:, :S]
        outs.append((cs_g - prev) / jnp.minimum(t1, float(w))[None, :, None])
    pooled = jnp.stack(outs, axis=2)
    return (pooled - vf).astype(v.dtype)


def even_mixer(h, w_in, conv_a, pool_w, pool_scale, w_out):
    u = jnp.einsum('bsd,dc->bsc', h, w_in)
    gate_b, gate_c, val_a, val_b = jnp.split(
        u, [A_WIDTH, 2 * A_WIDTH, 3 * A_WIDTH], axis=-1)
    y_a = gate_b * causal_dwconv(gate_c * val_a, conv_a)
    vg = val_b.reshape(val_b.shape[0], val_b.shape[1], B_GROUPS, B_GROUP_DIM)
    pg = causal_multiscale_pool(vg)
    y_b = jnp.einsum('bsgi,gio->bsgo', pg, pool_w).reshape(val_b.shape) * pool_scale
    y = jnp.concatenate([y_a, y_b], axis=-1)
    return jnp.einsum('bsc,cd->bsd', y, w_out)


def conformer_conv(h, w_pw1, b_pw1, w_dw, b_dw, ln_g, ln_b, w_pw2, b_pw2):
    u = jnp.einsum('bsd,dc->bsc', h, w_pw1) + b_pw1
    a, g = jnp.split(u, 2, axis=-1)
    u = a * jax.nn.sigmoid(g)
    u = causal_dwconv(u, w_dw) + b_dw
    u = jax.nn.silu(layer_norm(u, ln_g, ln_b))
    return jnp.einsum('bsc,cd->bsd', u, w_pw2) + b_pw2


def conv_ffn(h, w_up, conv_w, w_down):
    u = jnp.einsum('bsd,df->bsf', h, w_up)
    u = causal_dwconv(u, conv_w)
    g, v = jnp.split(u, 2, axis=-1)
    return jnp.einsum('bsf,fd->bsd', jax.nn.silu(g) * v, w_down)


def setup_inputs(seed: int = 0) -> dict:
    key = jax.random.key(seed)
    ks = jax.random.split(key, 24)
    f32 = jnp.float32
    nrm = lambda k, shape, scale: jax.random.normal(k, shape, f32) * scale
    gain = lambda k, shape: 1.0 + 0.05 * jax.random.normal(k, shape, f32)
    D = D_MODEL
    return {
        "x": nrm(ks[0], (BATCH, SEQ, D), 1.0),
        "meta_tokens": nrm(ks[1], (N_META, D), 1.0),
        "mix_pre_g": gain(ks[2], (DEPTH, D)),
        "mix_post_g": gain(ks[3], (DEPTH, D)),
        "ffn_pre_g": gain(ks[4], (DEPTH, D)),
        "ffn_post_g": gain(ks[5], (DEPTH, D)),
        "ab_w_in": nrm(ks[6], (N_EVEN, D, IN0_COLS), D ** -0.5),
        "ab_conv_w": nrm(ks[7], (N_EVEN, A_CONV, A_WIDTH), A_CONV ** -0.5),
        "ab_pool_w": nrm(ks[8], (N_EVEN, B_GROUPS, B_GROUP_DIM, B_GROUP_DIM), B_GROUP_DIM ** -0.5),
        "ab_pool_scale": gain(ks[9], (N_EVEN, B_WIDTH)),
        "ab_w_out": nrm(ks[10], (N_EVEN, MIX0_WIDTH, D), MIX0_WIDTH ** -0.5),
        "c_w_pw1": nrm(ks[11], (N_ODD, D, 2 * C_WIDTH), D ** -0.5),
        "c_b_pw1": nrm(ks[12], (N_ODD, 2 * C_WIDTH), 0.02),
        "c_w_dw": nrm(ks[13], (N_ODD, C_CONV, C_WIDTH), C_CONV ** -0.5),
        "c_b_dw": nrm(ks[14], (N_ODD, C_WIDTH), 0.02),
        "c_ln_g": gain(ks[15], (N_ODD, C_WIDTH)),
        "c_ln_b": nrm(ks[16], (N_ODD, C_WIDTH), 0.02),
        "c_w_pw2": nrm(ks[17], (N_ODD, C_WIDTH, D), C_WIDTH ** -0.5),
        "c_b_pw2": nrm(ks[18], (N_ODD, D), 0.02),
        "ffn_w_up": nrm(ks[19], (DEPTH, D, 2 * D_FF), D ** -0.5),
        "ffn_conv_w": nrm(ks[20], (DEPTH, FFN_CONV, 2 * D_FF), FFN_CONV ** -0.5),
        "ffn_w_down": nrm(ks[21], (DEPTH, D_FF, D), D_FF ** -0.5),
    }


def reference(x, meta_tokens, mix_pre_g, mix_post_g, ffn_pre_g, ffn_post_g,
              ab_w_in, ab_conv_w, ab_pool_w, ab_pool_scale, ab_w_out,
              c_w_pw1, c_b_pw1, c_w_dw, c_b_dw, c_ln_g, c_ln_b, c_w_pw2, c_b_pw2,
              ffn_w_up, ffn_conv_w, ffn_w_down):
    B = x.shape[0]
    meta = jnp.broadcast_to(meta_tokens.astype(x.dtype)[None], (B, N_META, x.shape[-1]))
    h = jnp.concatenate([meta, x], axis=1)
    for layer in range(DEPTH):
        z = rms_norm(h, mix_pre_g[layer])
        if layer % 2 == 0:
            i = layer // 2
            m = even_mixer(z, ab_w_in[i], ab_conv_w[i], ab_pool_w[i],
                           ab_pool_scale[i], ab_w_out[i])
        else:
            i = layer // 2
            m = conformer_conv(z, c_w_pw1[i], c_b_pw1[i], c_w_dw[i], c_b_dw[i],
                               c_ln_g[i], c_ln_b[i], c_w_pw2[i], c_b_pw2[i])
        h = h + rms_norm(m, mix_post_g[layer])
        z = rms_norm(h, ffn_pre_g[layer])
        f = conv_ffn(z, ffn_w_up[layer], ffn_conv_w[layer], ffn_w_down[layer])
        h = h + rms_norm(f, ffn_post_g[layer])
    return h[:, N_META:]
```

```python
import numpy as np
from contextlib import ExitStack
import concourse.bass as bass
import concourse.mybir as mybir
from concourse.bass_utils import run_bass_kernel_spmd

F32 = mybir.dt.float32
BF16 = mybir.dt.bfloat16
ALU = mybir.AluOpType
AF = mybir.ActivationFunctionType

D = 2048
KD = 16
SEQ = 4096
NMETA = 16
HALO = 56
CH = 1024
WIN = CH + HALO
TH = WIN // 2
NT = TH // 2
NTS = ((0, NT), (NT, NT))
DFF = 5632
KF = 44
EPS = 1e-6
NSLOT = 5
COMPUTE = ("pe", "act", "dve", "pool")


class Op:
    __slots__ = ("eng", "fn", "deps", "signal", "count", "sem", "is_dma", "idx")

    def __init__(self, eng, fn, is_dma):
        self.eng = eng
        self.fn = fn
        self.deps = []
        self.signal = False
        self.count = None
        self.sem = None
        self.is_dma = is_dma


class Prog:
    def __init__(self, nc, dry=False):
        self.nc = nc
        self.dry = dry
        self.ops = []
        self.lastw = {}
        self.readers = {}
        self.eng_sems = {}

    def op(self, eng, fn, reads=(), writes=(), dma=None):
        if self.dry:
            return None
        o = Op(eng, fn, dma is not None)
        o.idx = len(self.ops)
        deps = {}
        for r in reads:
            w = self.lastw.get(r)
            if w is not None:
                deps[w.idx] = (w, "raw")
        for wkey in writes:
            w = self.lastw.get(wkey)
            if w is not None and w.idx not in deps:
                deps[w.idx] = (w, "waw")
            for rd in self.readers.get(wkey, ()):
                if rd.idx not in deps:
                    deps[rd.idx] = (rd, "war")
        for d, kind in deps.values():
            if d.eng == eng and not d.is_dma and not o.is_dma:
                if eng == "pe":
                    continue
                if kind == "war":
                    continue
            o.deps.append(d)
            d.signal = True
        for wkey in writes:
            self.lastw[wkey] = o
            self.readers[wkey] = []
        for r in reads:
            self.readers.setdefault(r, []).append(o)
        if dma is not None:
            o.sem = dma
            o.signal = True
        self.ops.append(o)
        return o

    def emit(self, stack):
        nc = self.nc
        for e in COMPUTE:
            self.eng_sems[e] = stack.enter_context(nc.semaphore("s_" + e))
        dkeys = []
        seen = set()
        for o in self.ops:
            if o.is_dma and o.sem not in seen:
                seen.add(o.sem)
                dkeys.append(o.sem)
        dsem = {k: stack.enter_context(nc.semaphore("d%d" % i)) for i, k in enumerate(dkeys)}
        cnt = {e: 0 for e in COMPUTE}
        dcnt = {k: 0 for k in dkeys}
        for o in self.ops:
            if o.is_dma:
                dcnt[o.sem] += 16
                o.count = dcnt[o.sem]
                o.sem = dsem[o.sem]
            elif o.signal:
                cnt[o.eng] += 1
                o.count = cnt[o.eng]
                o.sem = self.eng_sems[o.eng]
        block = stack.enter_context(nc.Block())
        by_eng = {}
        for o in self.ops:
            by_eng.setdefault(o.eng, []).append(o)

        def run(engname, eng):
            waited = {}
            for o in by_eng.get(engname, ()):
                need = {}
                for d in o.deps:
                    k = id(d.sem)
                    if need.get(k, (None, 0))[1] < d.count:
                        need[k] = (d.sem, d.count)
                for k, (sem, c) in need.items():
                    if waited.get(k, 0) >= c:
                        continue
                    eng.wait_ge(sem, c)
                    waited[k] = c
                if o.fn is None:
                    continue
                ins = o.fn(eng)
                if o.is_dma:
                    ins.then_inc(o.sem, 16)
                elif o.signal:
                    ins.then_inc(o.sem, 1)

        @block.tensor
        def _(e):
            run("pe", e)

        @block.scalar
        def _(e):
            run("act", e)

        @block.vector
        def _(e):
            run("dve", e)

        @block.gpsimd
        def _(e):
            run("pool", e)

        @block.sync
        def _(e):
            run("sp", e)


def _blocks(Wm):
    K, C = Wm.shape
    nk, M = K // 128, C // 128
    return np.ascontiguousarray(
        Wm.reshape(nk, 128, M, 128).transpose(2, 1, 0, 3).reshape(M * 128, nk * 128))


def _col(v):
    return np.ascontiguousarray(np.asarray(v, np.float32).reshape(-1, 128).T)


class PPLayout:
    def __init__(self):
        self.off = {}
        self.n = 0

    def add(self, name, n):
        self.off[name] = self.n
        self.n += n


def _pp_layout():
    L = PPLayout()
    for l in range(2):
        for nm in ("mix_pre", "mix_post", "ffn_pre", "ffn_post"):
            L.add("%s%d" % (nm, l), 16)
    L.add("ab_conv", 24)
    L.add("ab_pscale", 8)
    L.add("rden15", 60)
    L.add("b_pw1", 32)
    L.add("wst", 32 * 16)
    L.add("b_dw", 16)
    L.add("ln_g", 16)
    L.add("ln_b", 16)
    L.add("b_pw2", 16)
    for l in range(2):
        L.add("ffn_conv%d" % l, 3 * 88)
    L.add("eps", 1)
    L.add("zero", 1)
    L.n = (L.n + 7) // 8 * 8
    return L


PPL = _pp_layout()


def build_nc(n_sub=4):
    nc = bass.Bass("TRN2", target_bir_lowering=False)
    dr = {}

    def din(name, shape):
        dr[name] = nc.dram_tensor(name, list(shape), F32, kind="ExternalInput").ap()
        return dr[name]

    din("xT", [128, KD * WIN])
    din("pp", [128, PPL.n])
    din("ident", [128, 32])
    din("poolw", [128, 4 * 2 * 256])
    din("w_in", [32 * 128, D])
    din("w_out", [16 * 128, D])
    din("pw1", [32 * 128, D])
    din("pw2", [16 * 128, D])
    for l in range(2):
        din("up%d" % l, [88 * 128, D])
        din("dn%d" % l, [16 * 128, DFF])
    outT = nc.dram_tensor("outT", [128, KD * WIN], F32, kind="ExternalOutput").ap()

    with ExitStack() as st:
        sizes = [
            ("h", KD * WIN * 4), ("pp", PPL.n * 4), ("zT", KD * TH * 2), ("R1", KF * TH * 2),
            ("M", KD * TH * 4), ("ring", NSLOT * 4096), ("rstd", TH * 4), ("mu", TH * 4), ("rstdA", TH * 4), ("sqC", 2 * TH * 2),
            ("identb", 64), ("onesb", 256), ("poolw", 4096),
            ("st_ffn", 2 * KF * 2 * 4), ("st_pa", 8 * 2 * 4), ("st_vb", 8 * 15 * 4), ("st_c", 16 * 30 * 2),
        ]
        offs = {}
        o = 0
        for nm, sz in sizes:
            offs[nm] = o
            o += (sz + 31) // 32 * 32
        total = o
        arena = st.enter_context(nc.sbuf_tensor("arena", [128, total // 4], F32))

        def view(off, nbytes, dt):
            a = arena[:, off // 4:(off + nbytes) // 4]
            return a if dt == F32 else a.bitcast(dt)

        hT = view(offs["h"], KD * WIN * 4, F32).rearrange("p (k t) -> p k t", k=KD)
        ppt = view(offs["pp"], PPL.n * 4, F32)
        zT = view(offs["zT"], KD * TH * 2, BF16).rearrange("p (k t) -> p k t", k=KD)
        yF = view(offs["R1"], KF * TH * 2, BF16).rearrange("p (k t) -> p k t", k=KF)
        cR = view(offs["R1"], KD * TH * 4, F32).rearrange("p (k t) -> p k t", k=KD)
        mH = view(offs["M"], KD * TH * 4, F32).rearrange("p (k t) -> p k t", k=KD)
        ring = [view(offs["ring"] + s * 4096, 4096, BF16).rearrange("p (k c) -> p k c", k=16)
                for s in range(NSLOT)]
        rstd = view(offs["rstd"], TH * 4, F32)
        mu = view(offs["mu"], TH * 4, F32)
        rstdA = view(offs["rstdA"], TH * 4, F32)
        sqC = [view(offs["sqC"] + i * TH * 2, TH * 2, BF16) for i in range(2)]
        sqA = [view(offs["mu"] + i * TH * 2, TH * 2, BF16) for i in range(2)]
        identb = view(offs["identb"], 64, BF16)
        onesb = view(offs["onesb"], 256, BF16)
        poolwb = view(offs["poolw"], 4096, BF16).rearrange("p (g i o) -> p g i o", g=4, i=2)
        st_ffn = view(offs["st_ffn"], 2 * KF * 2 * 4, F32).rearrange("p (a k t) -> p a k t", a=2, k=KF)
        st_pa = view(offs["st_pa"], 8 * 2 * 4, F32).rearrange("p (k t) -> p k t", k=8)
        st_vb = view(offs["st_vb"], 8 * 15 * 4, F32).rearrange("p (k t) -> p k t", k=8)
        st_c = view(offs["st_c"], 16 * 30 * 2, BF16).rearrange("p (k t) -> p k t", k=16)

        pst = [st.enter_context(nc.psum_tensor("ps%d" % i, [128, 2, 512], F32)) for i in range(4)]
        PROJ_T = (0, 1)
        S1, S2 = 2, 3

        def pcol(name, i=0):
            c = PPL.off[name] + i
            return ppt[:, c:c + 1]

        def v2(ap):
            return ap.rearrange("p (n t) -> p n t", n=2)

        def psv(i):
            return pst[i][:, :, 0:NT]

        MCH = TH * 4

        class Scr:
            reg = []

            def __init__(self, off, size):
                self.off, self.size = off, size
                self.key = ("sc", off)
                c0, c1 = off // MCH, (off + size - 1) // MCH
                self.mkeys = [("M", c) for c in range(c0, c1 + 1)]
                if not any(t.off == off and t.size == size for t in Scr.reg):
                    Scr.reg.append(self)

            def f32(self, n, o=0):
                return view(offs["M"] + self.off + o * 4, n * 4, F32)

            def bf(self, n, o=0):
                return view(offs["M"] + self.off + o * 2, n * 2, BF16)

        RTAIL = (34 * TH * 2 + 31) // 32 * 32
        assert RTAIL + 4 * 2304 <= KF * TH * 2

        class RScr:
            reg = []

            def __init__(self, off, size, ylo=34, extra=()):
                self.off, self.size = off, size
                self.key = ("rs", off, size)
                self.extra = list(extra)
                self.mkeys = [("y", k) for k in range(ylo, KF)]
                for t2 in RScr.reg:
                    if t2.off < off + size and t2.off + t2.size > off and t2.key != self.key:
                        self.mkeys += [t2.key] + t2.extra
                        t2.mkeys += [self.key] + self.extra
                if not any(t2.key == self.key for t2 in RScr.reg):
                    RScr.reg.append(self)

            def f32(self, n, o=0):
                return view(offs["R1"] + self.off + o * 4, n * 4, F32)

            def bf(self, n, o=0):
                return view(offs["R1"] + self.off + o * 2, n * 2, BF16)

        def gen_tiles(base, n, size=2304):
            return [Scr(base + i * size, size) for i in range(n)]

        def program(P, dry):
            wplan = program.wplan
            state = {"wi": 0, "ps": 0, "fresh": set(), "pre_done": False, "nxt": None, "tail": None,
                     "ptiles": PROJ_T}

            def issue(bi, extra_reads=()):
                if bi >= len(wplan):
                    return
                src, nk = wplan[bi]
                slot = bi % NSLOT
                P.op("pool", lambda e: e.dma_start(out=ring[slot][:, 0:nk, :], in_=src(),
                                                    max_dma_last_dim=8192),
                     reads=list(extra_reads), writes=[("w", slot)], dma=("w", slot))

            def load_x(half, extra_reads):
                xv = dr["xT"].rearrange("p (k t) -> p k t", k=KD)
                for kk in range(4):
                    P.op("sp", lambda e, half=half, kk=kk: e.dma_start(
                        out=hT[:, kk * 4:(kk + 1) * 4, half * TH:(half + 1) * TH],
                        in_=xv[:, kk * 4:(kk + 1) * 4, half * TH:(half + 1) * TH]),
                        reads=list(extra_reads),
                        writes=[("h", k, half) for k in range(kk * 4, kk * 4 + 4)], dma=("x", half, kk))

            def wget(src, nk):
                bi = state["wi"]
                state["wi"] += 1
                if dry:
                    wplan.append((src, nk))
                    return 0
                if bi == 0:
                    hk0 = [("h", k, 0) for k in range(KD)]
                    for j in range(NSLOT):
                        issue(j, hk0)
                    load_x(1, [("w", sl) for sl in range(NSLOT)])
                else:
                    issue(bi + NSLOT - 1)
                return bi % NSLOT

            def wsrc(name, m, k0, nk):
                return lambda: dr[name][m * 128:(m + 1) * 128, k0 * 128:(k0 + nk) * 128].rearrange(
                    "p (k c) -> p k c", c=128)

            def psnext():
                tl = state["ptiles"]
                i = tl[state["ps"] % len(tl)]
                state["ps"] += 1
                return i

            def drain_tail(chunks):
                tl = state["tail"]
                if tl is None:
                    return
                if chunks == KD:
                    chunks = range(KD)
                for c in sorted(set(chunks) & tl[1]):
                    tl[0](c)
                    tl[1].discard(c)
                if not tl[1]:
                    state["tail"] = None

            def scw(t):
                if t.key in state["fresh"]:
                    return [t.key]
                state["fresh"].add(t.key)
                drain_tail([mk[1] for mk in t.mkeys])
                return [t.key] + t.mkeys

            def proj(parts, rhs, rkey, first_extra=()):
                pt = psnext()
                ktot = sum(p[3] for p in parts)
                kk = 0
                for (name, m, k0, nk) in parts:
                    slot = wget(wsrc(name, m, k0, nk), nk)
                    for k in range(nk):
                        for nt, (o, n) in enumerate(NTS):
                            P.op("pe", (lambda e, slot=slot, k=k, nt=nt, o=o, n=n, kk=kk, pt=pt, kg=k0 + k:
                                        e.matmul(pst[pt][:, nt, 0:n], lhsT=ring[slot][:, k, :],
                                                 rhs=rhs(kg)[:, o:o + n],
                                                 start=(kk == 0), stop=(kk == ktot - 1))),
                                 reads=[("w", slot), rkey(k0 + k)] + (list(first_extra) if kk == 0 else []),
                                 writes=[("ps", pt)])
                        kk += 1
                return pt

            def ones_mm(tile, src_ap, skey, first, last):
                for nt, (o, n) in enumerate(NTS):
                    P.op("pe", (lambda e, nt=nt, o=o, n=n:
                                e.matmul(pst[tile][:, nt, 0:n], lhsT=onesb, rhs=src_ap[:, o:o + n],
                                         start=first, stop=last)),
                         reads=[skey, "onesb"], writes=[("ps", tile)])

            def rstd_from(tile, dst, dkey):
                P.op("act", lambda e: e.activation(out=v2(dst), in_=psv(tile), func=AF.Ln,
                                                   scale=1.0 / D, bias=pcol("eps")),
                     reads=[("ps", tile), "pp"], writes=[dkey])
                P.op("act", lambda e: e.activation(out=dst, in_=dst, func=AF.Exp, scale=-0.5),
                     reads=[dkey], writes=[dkey])

            def A_sq(k, half):
                t0 = half * TH
                i = k % 2
                wk = [("sqA", i)] + (["mu"] if k < 2 else [])
                P.op("act", lambda e, k=k, i=i: e.activation(out=sqA[i], in_=hT[:, k, t0:t0 + TH],
                                                              func=AF.Square),
                     reads=[("h", k, half)], writes=wk)

            def A_add(p):
                P.op("dve", lambda e: e.tensor_tensor(out=sqA[0], in0=sqA[0], in1=sqA[1], op=ALU.add),
                     reads=[("sqA", 0), ("sqA", 1)], writes=[("sqA", 0)])

            def A_pmm(p):
                ones_mm(S2, sqA[0], ("sqA", 0), p == 0, p == KD // 2 - 1)

            def A_pair(p):
                A_add(p)
                A_pmm(p)

            def A_fin():
                rstd_from(S2, rstdA, "rstdA")

            def A_z(k, half, gname):
                t0 = half * TH
                P.op("dve", lambda e, k=k: e.scalar_tensor_tensor(
                    out=zT[:, k, :], in0=hT[:, k, t0:t0 + TH], scalar=pcol(gname, k), in1=rstdA,
                    op0=ALU.mult, op1=ALU.mult),
                    reads=[("h", k, half), "rstdA", "pp"], writes=[("zT", k)])

            def phaseA(half, gname):
                if state["pre_done"]:
                    state["pre_done"] = False
                    return
                for k in range(KD):
                    A_sq(k, half)
                    ones_mm(S2, sqA[k % 2], ("sqA", k % 2), k == 0, k == KD - 1)
                A_fin()
                for k in range(KD):
                    A_z(k, half, gname)

            def phaseC(half, wname, nkc, rhs, rkey, gname, bias_name, sq_tiles, overlap=True, first_extra=()):
                t0 = half * TH
                parts_of = lambda m: [(wname, m, k0, min(16, nkc - k0)) for k0 in range(0, nkc, 16)]
                pend = None
                state["ptiles"] = PROJ_T
                nxt = state["nxt"] if overlap else None
                for m in range(KD):
                    if m == 0:
                        drain_tail(KD)
                    if nxt is not None:
                        nh, ng = nxt
                        if 1 <= m <= 8:
                            A_pmm(m - 1)
                        if m < 8:
                            A_sq(2 * m, nh)
                            A_sq(2 * m + 1, nh)
                            A_add(m)
                        else:
                            if m == 8:
                                A_fin()
                            A_z(2 * (m - 8), nh, ng)
                            A_z(2 * (m - 8) + 1, nh, ng)
                    pt = proj(parts_of(m), rhs, rkey, first_extra if m == 0 else ())
                    if pend is not None:
                        ones_mm(S1, pend[0], pend[1], pend[2] == 0, False)
                        pend = None
                    b = pcol(bias_name, m) if bias_name else 0.0
                    alias = [t.key for t in Scr.reg
                             if t.off < (m + 1) * MCH and t.off + t.size > m * MCH]
                    P.op("act", lambda e, m=m, pt=pt, b=b: e.activation(
                        out=v2(mH[:, m, :]), in_=psv(pt), func=AF.Identity, bias=b),
                        reads=[("ps", pt), "pp"], writes=[("M", m)] + alias)
                    sq_ap, sq_key = sq_tiles[m % 2]
                    P.op("act", lambda e, pt=pt, b=b, sq_ap=sq_ap: e.activation(
                        out=v2(sq_ap), in_=psv(pt), func=AF.Square, bias=b),
                        reads=[("ps", pt), "pp"], writes=[sq_key])
                    if m % 2 == 1:
                        (s0, k0_), (s1, k1_) = sq_tiles[0], sq_tiles[1]
                        P.op("dve", lambda e, s0=s0, s1=s1: e.tensor_tensor(out=s0, in0=s0, in1=s1, op=ALU.add),
                             reads=[k0_, k1_], writes=[k0_])
                        pend = (s0, k0_, m // 2)
                ones_mm(S1, pend[0], pend[1], False, True)
                if nxt is not None:
                    state["pre_done"] = True
                rstd_from(S1, rstd, "rstd")

                def tail(m):
                    P.op("dve", lambda e, m=m: e.scalar_tensor_tensor(
                        out=mH[:, m, :], in0=mH[:, m, :], scalar=pcol(gname, m), in1=rstd,
                        op0=ALU.mult, op1=ALU.mult),
                        reads=[("M", m), "rstd", "pp"], writes=[("M", m)])
                    P.op("dve", lambda e, m=m: e.tensor_tensor(
                        out=hT[:, m, t0:t0 + TH], in0=hT[:, m, t0:t0 + TH], in1=mH[:, m, :], op=ALU.add),
                        reads=[("M", m), ("h", m, half)], writes=[("h", m, half)])
                state["tail"] = [tail, set(range(KD))]

            zsq = [(sqC[i], ("sqC", i)) for i in range(2)]

            def ffn(l, half):
                state["fresh"] = set()
                phaseA(half, "ffn_pre%d" % l)
                state["ptiles"] = (0, 1, S2)
                tilesB = gen_tiles(0, 4)
                tilesAM = gen_tiles(4 * 2304, 4)
                tilesAR = [RScr(RTAIL + j * 2304, 2304) for j in range(4)]
                up = "up%d" % l
                cw = "ffn_conv%d" % l
                for i in range(KF):
                    r = i % 2
                    ug, tg, uv, tv = tilesB if r == 1 else (tilesAR if i < 30 else tilesAM)
                    if 8 <= i < 20:
                        drain_tail([i - 4])
                    res = []
                    for which, (u_t, t_t, blk) in enumerate(((ug, tg, i), (uv, tv, KF + i))):
                        pt = proj([(up, blk, 0, 16)], lambda k: zT[:, k, :], lambda k: ("zT", k))
                        u = u_t.f32(TH + 2)
                        t = t_t.f32(TH)
                        if half == 0:
                            P.op("pool", lambda e, u=u: e.memset(u[:, 0:2], 0.0), writes=scw(u_t))
                        else:
                            P.op("pool", lambda e, u=u, which=which, i=i: e.tensor_copy(
                                out=u[:, 0:2], in_=st_ffn[:, which, i, :]),
                                reads=[("st_ffn", which, i)], writes=scw(u_t))
                        P.op("act", lambda e, u=u, pt=pt: e.activation(
                            out=v2(u[:, 2:TH + 2]), in_=psv(pt), func=AF.Copy),
                            reads=[("ps", pt)], writes=[u_t.key])
                        P.op("act", lambda e, t=t, pt=pt, blk=blk: e.activation(
                            out=v2(t), in_=psv(pt), func=AF.Copy, scale=pcol(cw, 2 * 88 + blk)),
                            reads=[("ps", pt), "pp"], writes=scw(t_t))
                        if half == 0:
                            P.op("pool", lambda e, u=u, which=which, i=i: e.tensor_copy(
                                out=st_ffn[:, which, i, :], in_=u[:, TH:TH + 2]),
                                reads=[u_t.key], writes=[("st_ffn", which, i)])
                        P.op("dve", lambda e, u=u, t=t, blk=blk: e.scalar_tensor_tensor(
                            out=t, in0=u[:, 1:TH + 1], scalar=pcol(cw, 88 + blk), in1=t,
                            op0=ALU.mult, op1=ALU.add),
                            reads=[u_t.key, t_t.key, "pp"], writes=[t_t.key])
                        P.op("dve", lambda e, u=u, t=t, blk=blk: e.scalar_tensor_tensor(
                            out=t, in0=u[:, 0:TH], scalar=pcol(cw, blk), in1=t,
                            op0=ALU.mult, op1=ALU.add),
                            reads=[u_t.key, t_t.key, "pp"], writes=[t_t.key])
                        res.append(t)
                    P.op("act", lambda e, tgv=res[0]: e.activation(out=tgv, in_=tgv, func=AF.Silu),
                         reads=[tg.key], writes=[tg.key])
                    P.op("dve", lambda e, tgv=res[0], tvv=res[1], i=i: e.tensor_tensor(
                        out=yF[:, i, :], in0=tgv, in1=tvv, op=ALU.mult),
                        reads=[tg.key, tv.key],
                        writes=[("y", i)] + [kk_ for t in RScr.reg
                                             if t.off < (i + 1) * TH * 2 and t.off + t.size > i * TH * 2
                                             for kk_ in [t.key] + t.extra])
                phaseC(half, "dn%d" % l, KF, lambda k: yF[:, k, :], lambda k: ("y", k),
                       "ffn_post%d" % l, None, zsq)

            def mixer0(half):
                state["fresh"] = set()
                phaseA(half, "mix_pre0")
                state["ptiles"] = (0, 1, S2)
                RM0 = (KD * TH * 2 + 31) // 32 * 32
                assert RM0 + 12 * 2304 <= KF * TH * 2
                tiles = [RScr(RM0 + j_ * 2304, 2304, KD) for j_ in range(12)]
                zr = lambda k: zT[:, k, :]
                zk = lambda k: ("zT", k)
                for j in range(8):
                    r = j % 2
                    drain_tail([j])
                    gc_t, pa_t, t_t = tiles[r * 3:(r + 1) * 3]
                    gc = gc_t.f32(TH)
                    pa = pa_t.f32(TH + 2)
                    t = t_t.f32(TH)
                    p_gc = proj([("w_in", 8 + j, 0, 16)], zr, zk)
                    P.op("act", lambda e, gc=gc, p=p_gc: e.activation(out=v2(gc), in_=psv(p), func=AF.Copy),
                         reads=[("ps", p_gc)], writes=scw(gc_t))
                    p_va = proj([("w_in", 16 + j, 0, 16)], zr, zk)
                    if half == 0:
                        P.op("pool", lambda e, pa=pa: e.memset(pa[:, 0:2], 0.0), writes=scw(pa_t))
                    else:
                        P.op("pool", lambda e, pa=pa, j=j: e.tensor_copy(out=pa[:, 0:2], in_=st_pa[:, j, :]),
                             reads=[("st_pa", j)], writes=scw(pa_t))
                    P.op("dve", lambda e, pa=pa, gc=gc, p=p_va: e.tensor_tensor(
                        out=v2(pa[:, 2:TH + 2]), in0=v2(gc), in1=psv(p), op=ALU.mult),
                        reads=[gc_t.key, ("ps", p_va)], writes=[pa_t.key])
                    if half == 0:
                        P.op("pool", lambda e, pa=pa, j=j: e.tensor_copy(out=st_pa[:, j, :], in_=pa[:, TH:TH + 2]),
                             reads=[pa_t.key], writes=[("st_pa", j)])
                    P.op("dve", lambda e, pa=pa, t=t, j=j: e.tensor_scalar(
                        out=t, in0=pa[:, 2:TH + 2], scalar1=pcol("ab_conv", 16 + j), scalar2=None, op0=ALU.mult),
                        reads=[pa_t.key, "pp"], writes=scw(t_t))
                    P.op("dve", lambda e, pa=pa, t=t, j=j: e.scalar_tensor_tensor(
                        out=t, in0=pa[:, 1:TH + 1], scalar=pcol("ab_conv", 8 + j), in1=t,
                        op0=ALU.mult, op1=ALU.add), reads=[pa_t.key, t_t.key, "pp"], writes=[t_t.key])
                    P.op("dve", lambda e, pa=pa, t=t, j=j: e.scalar_tensor_tensor(
                        out=t, in0=pa[:, 0:TH], scalar=pcol("ab_conv", j), in1=t,
                        op0=ALU.mult, op1=ALU.add), reads=[pa_t.key, t_t.key, "pp"], writes=[t_t.key])
                    p_gb = proj([("w_in", j, 0, 16)], zr, zk)
                    P.op("dve", lambda e, t=t, p=p_gb, j=j: e.tensor_tensor(
                        out=v2(yF[:, j, :]), in0=v2(t), in1=psv(p), op=ALU.mult),
                        reads=[t_t.key, ("ps", p_gb)], writes=[("y", j)])
                HL = 15
                EL = TH + HL
                pgbase = 12 * 2304
                pgt = [[Scr(pgbase + (2 * s_ + ic) * 1088, 1088) for ic in range(2)] for s_ in range(2)]
                tmp_t = Scr(pgbase + 4 * 1088, 64)

                def pool_mm(g):
                    pgs = [pgt[g % 2][ic].bf(TH) for ic in range(2)]
                    for oc in range(2):
                        pt = psnext()
                        for ic in range(2):
                            for nt, (o, n) in enumerate(NTS):
                                P.op("pe", (lambda e, pt=pt, nt=nt, o=o, n=n, ic=ic, oc=oc, g=g, pgs=pgs:
                                            e.matmul(pst[pt][:, nt, 0:n],
                                                     lhsT=poolwb[:, g, ic, oc * 128:(oc + 1) * 128],
                                                     rhs=pgs[ic][:, o:o + n], start=(ic == 0), stop=(ic == 1))),
                                     reads=[pgt[g % 2][ic].key, "poolw"], writes=[("ps", pt)])
                        P.op("act", lambda e, pt=pt, g=g, oc=oc: e.activation(
                            out=v2(yF[:, 8 + 2 * g + oc, :]), in_=psv(pt), func=AF.Copy,
                            scale=pcol("ab_pscale", 2 * g + oc)),
                            reads=[("ps", pt), "pp"], writes=[("y", 8 + 2 * g + oc)])
                for c in range(8):
                    g = c // 2
                    drain_tail([8 + c])
                    wdw = 2 << g
                    r = c % 2
                    e_t, sa_t, sb_t = tiles[6 + r * 3:6 + (r + 1) * 3]
                    pg_t = pgt[g % 2][c % 2]
                    ee = e_t.f32(EL)
                    sa = sa_t.f32(EL)
                    sb = sb_t.f32(EL)
                    p_vb = proj([("w_in", 24 + c, 0, 16)], zr, zk)
                    if half == 0:
                        P.op("pool", lambda e, ee=ee: e.memset(ee[:, 0:HL], 0.0), writes=scw(e_t))
                    else:
                        P.op("pool", lambda e, ee=ee, c=c: e.tensor_copy(out=ee[:, 0:HL], in_=st_vb[:, c, :]),
                             reads=[("st_vb", c)], writes=scw(e_t))
                    P.op("act", lambda e, ee=ee, p=p_vb: e.activation(
                        out=v2(ee[:, HL:EL]), in_=psv(p), func=AF.Copy),
                        reads=[("ps", p_vb)], writes=[e_t.key])
                    if half == 0:
                        P.op("pool", lambda e, ee=ee, c=c: e.tensor_copy(out=st_vb[:, c, :], in_=ee[:, TH:EL]),
                             reads=[e_t.key], writes=[("st_vb", c)])
                    if c % 2 == 1 and g >= 1:
                        pool_mm(g - 1)
                    src, src_t = ee, e_t
                    sh = 1
                    lo = 0
                    bufs = [(sa, sa_t), (sb, sb_t)]
                    bi = 0
                    while sh < wdw:
                        dst, dst_t = bufs[bi]
                        bi ^= 1
                        lo2 = lo + sh
                        P.op("dve", lambda e, dst=dst, src=src, lo2=lo2, sh=sh: e.tensor_tensor(
                            out=dst[:, lo2:EL], in0=src[:, lo2:EL], in1=src[:, lo2 - sh:EL - sh], op=ALU.add),
                            reads=[src_t.key], writes=scw(dst_t))
                        src, src_t = dst, dst_t
                        lo = lo2
                        sh *= 2
                    pgb = pg_t.bf(TH)
                    P.op("dve", lambda e, pgb=pgb, src=src, ee=ee, wdw=wdw: e.scalar_tensor_tensor(
                        out=pgb, in0=src[:, HL:EL], scalar=1.0 / wdw, in1=ee[:, HL:EL],
                        op0=ALU.mult, op1=ALU.subtract),
                        reads=[src_t.key, e_t.key], writes=scw(pg_t))
                    if half == 0:
                        tmp = tmp_t.f32(HL)
                        P.op("dve", lambda e, tmp=tmp, src=src, g=g: e.tensor_tensor(
                            out=tmp, in0=src[:, HL:2 * HL],
                            in1=ppt[:, PPL.off["rden15"] + g * 15:PPL.off["rden15"] + (g + 1) * 15], op=ALU.mult),
                            reads=[src_t.key, "pp"], writes=scw(tmp_t))
                        P.op("dve", lambda e, tmp=tmp, pgb=pgb, ee=ee: e.tensor_tensor(
                            out=pgb[:, 0:HL], in0=tmp, in1=ee[:, HL:2 * HL], op=ALU.subtract),
                            reads=[tmp_t.key, pg_t.key, e_t.key], writes=[pg_t.key])
                pool_mm(3)
                phaseC(half, "w_out", 16, lambda k: yF[:, k, :], lambda k: ("y", k), "mix_post0", None, zsq)

            def mixer1(half):
                state["fresh"] = set()
                phaseA(half, "mix_pre1")
                state["ptiles"] = PROJ_T
                yC = lambda k: cR[:, k, 0:NT].bitcast(BF16)
                XL = TH + 28
                LB = 32 * 32 * 2
                XB = 4 * XL * 2
                tiles = gen_tiles(0, 8)
                TB = 8 * 2304
                RB = KD * TH * 4
                assert RB + 2 * XB + LB <= KF * TH * 2
                Xt = [RScr(RB + r_ * XB, XB, 32, [("X", r_, g, j) for g in range(4) for j in range(4)])
                      for r_ in range(2)]
                Lt = [RScr(RB + 2 * XB, LB, 32), Scr(TB, LB)]
                zr = lambda k: zT[:, k, :]
                zk = lambda k: ("zT", k)
                CL = TH + 30
                pend1 = None
                pend_conv = None
                for i in range(KD):
                    r = i % 2
                    sg_t, cin_t, cs_t = tiles[r * 3:(r + 1) * 3]
                    sg = sg_t.f32(TH)
                    cin = cin_t.bf(CL)
                    cb = cs_t.bf(TH)
                    sqb = cs_t.bf(TH, o=TH)
                    L_t, X_t = Lt[r], Xt[r]
                    Lb = L_t.bf(32 * 32).rearrange("p (a c) -> p a c", a=32)
                    Xb = X_t.bf(4 * XL).rearrange("p (g t) -> p g t", g=4)
                    xkeys = [("X", r, g, j) for g in range(4) for j in range(4)]
                    p_g = proj([("pw1", 16 + i, 0, 16)], zr, zk)
                    P.op("act", lambda e, sg=sg, p=p_g, i=i: e.activation(
                        out=v2(sg), in_=psv(p), func=AF.Sigmoid, bias=pcol("b_pw1", 16 + i)),
                        reads=[("ps", p_g), "pp"], writes=scw(sg_t))
                    p_a = proj([("pw1", i, 0, 16)], zr, zk)
                    if half == 0:
                        P.op("pool", lambda e, cin=cin: e.memset(cin[:, 0:30], 0.0), writes=scw(cin_t))
                    else:
                        P.op("pool", lambda e, cin=cin, i=i: e.tensor_copy(out=cin[:, 0:30], in_=st_c[:, i, :]),
                             reads=[("st_c", i)], writes=scw(cin_t))
                    P.op("dve", lambda e, cin=cin, sg=sg, p=p_a, i=i: e.scalar_tensor_tensor(
                        out=v2(cin[:, 30:CL]), in0=psv(p), scalar=pcol("b_pw1", i), in1=v2(sg),
                        op0=ALU.add, op1=ALU.mult),
                        reads=[("ps", p_a), sg_t.key, "pp"], writes=[cin_t.key])
                    if half == 0:
                        P.op("pool", lambda e, cin=cin, i=i: e.tensor_copy(out=st_c[:, i, :], in_=cin[:, TH:CL]),
                             reads=[cin_t.key], writes=[("st_c", i)])
                    wd = ppt[:, PPL.off["wst"] + i * 32:PPL.off["wst"] + (i + 1) * 32]
                    P.op("dve", lambda e, Lb=Lb, wd=wd: e.tensor_tensor(
                        out=Lb, in0=identb[:, None, :].broadcast_to([128, 32, 32]),
                        in1=wd[:, :, None].broadcast_to([128, 32, 32]), op=ALU.mult),
                        reads=["identb", "pp"], writes=scw(L_t))
                    P.op("pool", lambda e, Xb=Xb: e.memset(Xb[:, :, XL - 1:XL], 0.0),
                         writes=scw(X_t) + xkeys)
                    for g in range(4):
                        q = "sp" if g < 2 else "pool"
                        for j in range(4):
                            ncol = XL if j < 3 else XL - 1
                            P.op(q, lambda e, g=g, j=j, ncol=ncol, Xb=Xb, cin=cin: e.dma_start(
                                out=Xb[32 * j:32 * j + 32, g, 0:ncol], in_=cin[32 * g:32 * g + 32, j:j + ncol]),
                                reads=[cin_t.key], writes=[("X", r, g, j)], dma=("xs", r, q))

                    def conv_evac(i=i, r=r, Lb=Lb, Xb=Xb, L_t=L_t, xkeys=xkeys, cs_t=cs_t, cb=cb, sqb=sqb):
                        pt = psnext()
                        for b_ in range(8):
                            for nt, (o, n) in enumerate(NTS):
                                for g in range(4):
                                    P.op("pe", (lambda e, pt=pt, nt=nt, o=o, n=n, b_=b_, g=g:
                                                e.matmul(pst[pt][32 * g:32 * g + 32, nt, 0:n],
                                                         lhsT=Lb[:, g * 8 + b_, :],
                                                         rhs=Xb[:, g, o + 4 * b_:o + 4 * b_ + n],
                                                         start=(b_ == 0), stop=(b_ == 7),
                                                         tile_position=(0, 32 * g))),
                                         reads=[L_t.key] + xkeys, writes=[("ps", pt)])
                        P.op("act", lambda e, pt=pt: e.activation(
                            out=v2(cR[:, i, :]), in_=psv(pt), func=AF.Identity, bias=pcol("b_dw", i)),
                            reads=[("ps", pt), "pp"], writes=[("c", i)] + ([("csq", i)] if i < 4 else []))
                        P.op("act", lambda e, pt=pt: e.activation(
                            out=v2(cb), in_=psv(pt), func=AF.Identity, bias=pcol("b_dw", i)),
                            reads=[("ps", pt), "pp"], writes=scw(cs_t))
                        P.op("act", lambda e, pt=pt: e.activation(
                            out=v2(sqb), in_=psv(pt), func=AF.Square, bias=pcol("b_dw", i)),
                            reads=[("ps", pt), "pp"], writes=[cs_t.key])
                        return (cb, sqb, cs_t.key, i)

                    if pend_conv is not None:
                        if pend1 is not None:
                            ones_mm(S1, pend1[0], pend1[2], pend1[3] == 0, False)
                            ones_mm(S2, pend1[1], pend1[2], pend1[3] == 0, False)
                        pend1 = pend_conv()
                    pend_conv = conv_evac
                ones_mm(S1, pend1[0], pend1[2], False, False)
                ones_mm(S2, pend1[1], pend1[2], False, False)
                pend1 = pend_conv()
                ones_mm(S1, pend1[0], pend1[2], False, True)
                ones_mm(S2, pend1[1], pend1[2], False, True)
                drain_tail(KD)
                t6, t7 = tiles[6], tiles[7]
                ex2 = t6.f32(TH)
                msq = t7.f32(TH)
                P.op("act", lambda e: e.activation(out=v2(msq), in_=psv(S1), func=AF.Square, scale=1.0 / D),
                     reads=[("ps", S1)], writes=scw(t7))
                P.op("act", lambda e: e.activation(out=v2(mu), in_=psv(S1), func=AF.Copy, scale=1.0 / D),
                     reads=[("ps", S1)], writes=["mu", ("sqA", 0), ("sqA", 1)])
                P.op("dve", lambda e: e.scalar_tensor_tensor(out=v2(ex2), in0=psv(S2), scalar=1.0 / D, in1=v2(msq),
                                                             op0=ALU.mult, op1=ALU.subtract),
                     reads=[("ps", S2), t7.key], writes=scw(t6))
                P.op("act", lambda e: e.activation(out=rstd, in_=ex2, func=AF.Ln, scale=1.0, bias=pcol("eps")),
                     reads=[t6.key, "pp"], writes=["rstd"])
                P.op("act", lambda e: e.activation(out=rstd, in_=rstd, func=AF.Exp, scale=-0.5),
                     reads=["rstd"], writes=["rstd"])
                for i in range(KD):
                    lt_t = tiles[i % 2 * 3]
                    lt = lt_t.f32(TH)
                    P.op("dve", lambda e, lt=lt, i=i: e.tensor_tensor(out=lt, in0=cR[:, i, :], in1=mu, op=ALU.subtract),
                         reads=[("c", i), "mu"], writes=scw(lt_t))
                    P.op("dve", lambda e, lt=lt: e.tensor_tensor(out=lt, in0=lt, in1=rstd, op=ALU.mult),
                         reads=[lt_t.key, "rstd"], writes=[lt_t.key])
                    P.op("act", lambda e, lt=lt, i=i: e.activation(
                        out=yC(i), in_=lt, func=AF.Silu, scale=pcol("ln_g", i), bias=pcol("ln_b", i)),
                        reads=[lt_t.key, "pp"], writes=[("c", i)])
                csq = [(cR[:, j, NT:TH].bitcast(BF16), ("csq", j)) for j in range(4)]
                phaseC(half, "pw2", 16, yC, lambda k: ("c", k), "mix_post1", "b_pw2", csq,
                       first_extra=[("c", KD - 8)])

            P.op("sp", lambda e: e.dma_start(out=ppt, in_=dr["pp"]), writes=["pp"], dma="ld_pp")
            P.op("pool", lambda e: e.dma_start(out=identb, in_=dr["ident"]), writes=["identb"], dma="ld_id")
            P.op("pool", lambda e: e.dma_start(out=poolwb, in_=dr["poolw"].rearrange(
                "p (g i o) -> p g i o", g=4, i=2)), writes=["poolw"], dma="ld_pw")
            P.op("dve", lambda e: e.memset(onesb, 1.0), writes=["onesb"])
            xv = dr["xT"].rearrange("p (k t) -> p k t", k=KD)
            ov = outT.rearrange("p (k t) -> p k t", k=KD)
            load_x(0, ())
            subs = [lambda h: mixer0(h), lambda h: ffn(0, h), lambda h: mixer1(h), lambda h: ffn(1, h)]
            pre_g = ["mix_pre0", "ffn_pre0", "mix_pre1", "ffn_pre1"]
            steps = [(si, half) for si in range(n_sub) for half in range(2)]
            state["pre_done"] = False
            for n, (si, half) in enumerate(steps):
                if n + 1 < len(steps):
                    state["nxt"] = (steps[n + 1][1], pre_g[steps[n + 1][0]])
                else:
                    state["nxt"] = None
                subs[si](half)
            drain_tail(KD)
            okeys = []
            for half in range(2):
                for kk in range(4):
                    P.op("sp", lambda e, half=half, kk=kk: e.dma_start(
                        out=ov[:, kk * 4:(kk + 1) * 4, half * TH:(half + 1) * TH],
                        in_=hT[:, kk * 4:(kk + 1) * 4, half * TH:(half + 1) * TH]),
                        reads=[("h", k, half) for k in range(kk * 4, kk * 4 + 4)],
                        writes=[("out", half, kk)], dma=("o", half, kk))
                    okeys.append(("out", half, kk))
            P.op("sp", None, reads=okeys)

        program.wplan = []
        program(Prog(nc, dry=True), True)
        P = Prog(nc)
        program(P, False)
        P.emit(st)
    return nc


def _prep_shared(inp):
    f = lambda a: np.asarray(a, np.float32)
    sh = {}
    sh["w_in"] = _blocks(f(inp["ab_w_in"])[0])
    sh["w_out"] = _blocks(f(inp["ab_w_out"])[0])
    sh["pw1"] = _blocks(f(inp["c_w_pw1"])[0])
    sh["pw2"] = _blocks(f(inp["c_w_pw2"])[0])
    for l in range(2):
        sh["up%d" % l] = _blocks(f(inp["ffn_w_up"])[l])
        sh["dn%d" % l] = _blocks(f(inp["ffn_w_down"])[l])
    pw = f(inp["ab_pool_w"])[0]
    sh["poolw"] = np.ascontiguousarray(
        pw.reshape(4, 2, 128, 256).transpose(2, 0, 1, 3).reshape(128, 4 * 2 * 256))
    sh["ident"] = np.ascontiguousarray(np.tile(np.eye(32, dtype=np.float32), (4, 1)))
    pp = np.zeros((128, PPL.n), np.float32)

    def put(name, arr):
        pp[:, PPL.off[name]:PPL.off[name] + arr.shape[1]] = arr

    for l in range(2):
        put("mix_pre%d" % l, _col(f(inp["mix_pre_g"])[l]))
        put("mix_post%d" % l, _col(f(inp["mix_post_g"])[l]))
        put("ffn_pre%d" % l, _col(f(inp["ffn_pre_g"])[l]))
        put("ffn_post%d" % l, _col(f(inp["ffn_post_g"])[l]))
        cw = f(inp["ffn_conv_w"])[l]
        put("ffn_conv%d" % l, np.concatenate([_col(cw[t]) for t in range(3)], axis=1))
    acw = f(inp["ab_conv_w"])[0]
    put("ab_conv", np.concatenate([_col(acw[t]) for t in range(3)], axis=1))
    put("ab_pscale", _col(f(inp["ab_pool_scale"])[0]))
    rd = np.zeros((128, 60), np.float32)
    for g, w in enumerate((2, 4, 8, 16)):
        for p in range(15):
            rd[:, g * 15 + p] = np.float32(1.0) / np.float32(min(p + 1, w))
    put("rden15", rd)
    put("b_pw1", _col(f(inp["c_b_pw1"])[0]))
    wdw = f(inp["c_w_dw"])[0]
    w32 = np.concatenate([wdw, np.zeros((1, D), np.float32)], axis=0)
    wst = w32.reshape(8, 4, 16, 4, 32).transpose(1, 4, 2, 3, 0).reshape(128, 16 * 32)
    put("wst", np.ascontiguousarray(wst))
    put("b_dw", _col(f(inp["c_b_dw"])[0]))
    put("ln_g", _col(f(inp["c_ln_g"])[0]))
    put("ln_b", _col(f(inp["c_ln_b"])[0]))
    put("b_pw2", _col(f(inp["c_b_pw2"])[0]))
    pp[:, PPL.off["eps"]] = EPS
    sh["pp"] = pp
    return sh


def _windows(x, meta):
    B = x.shape[0]
    wins, offs = [], []
    for b in range(B):
        seq = np.concatenate([meta, x[b]], axis=0)
        for c in range(4):
            if c == 0:
                s, ooff = 0, NMETA
            else:
                s, ooff = NMETA + CH * c - HALO, HALO
            w = seq[s:s + WIN]
            wt = np.ascontiguousarray(w.T.reshape(KD, 128, WIN).transpose(1, 0, 2).reshape(128, KD * WIN))
            wins.append(wt)
            offs.append(ooff)
    return wins, offs


_NC_CACHE = {}


def kernel(**inputs):
    x = np.asarray(inputs["x"], np.float32)
    meta = np.asarray(inputs["meta_tokens"], np.float32)
    sh = _prep_shared(inputs)
    wins, offs = _windows(x, meta)
    if "nc" not in _NC_CACHE:
        _NC_CACHE["nc"] = build_nc()
    nc = _NC_CACHE["nc"]
    in_maps = []
    for c in range(8):
        m = dict(sh)
        m["xT"] = wins[c]
        in_maps.append(m)
    res = run_bass_kernel_spmd(nc, in_maps, core_ids=list(range(8)))
    out = np.empty((x.shape[0], SEQ, D), np.float32)
    for c in range(8):
        oT = res.results[c]["outT"].reshape(128, KD, WIN)
        tok = oT.transpose(2, 1, 0).reshape(WIN, D)
        b, q = divmod(c, 4)
        out[b, q * CH:(q + 1) * CH] = tok[offs[c]:offs[c] + CH]
    return out
```

```python
import numpy as np
from contextlib import ExitStack
import concourse.bass as bass
import concourse.mybir as mybir
from concourse.bass_utils import run_bass_kernel_spmd

F32 = mybir.dt.float32
BF16 = mybir.dt.bfloat16
ALU = mybir.AluOpType
AF = mybir.ActivationFunctionType

D = 2048
KD = 16
SEQ = 4096
NMETA = 16
HALO = 56
CH = 1024
WIN = CH + HALO
TH = WIN // 2
NT = TH // 2
NTS = ((0, NT), (NT, NT))
DFF = 5632
KF = 44
EPS = 1e-6
NSLOT = 5
COMPUTE = ("pe", "act", "dve", "pool")


class Op:
    __slots__ = ("eng", "fn", "deps", "signal", "count", "sem", "is_dma", "idx")

    def __init__(self, eng, fn, is_dma):
        self.eng = eng
        self.fn = fn
        self.deps = []
        self.signal = False
        self.count = None
        self.sem = None
        self.is_dma = is_dma


class Prog:
    def __init__(self, nc, dry=False):
        self.nc = nc
        self.dry = dry
        self.ops = []
        self.lastw = {}
        self.readers = {}
        self.eng_sems = {}

    def op(self, eng, fn, reads=(), writes=(), dma=None):
        if self.dry:
            return None
        o = Op(eng, fn, dma is not None)
        o.idx = len(self.ops)
        deps = {}
        for r in reads:
            w = self.lastw.get(r)
            if w is not None:
                deps[w.idx] = (w, "raw")
        for wkey in writes:
            w = self.lastw.get(wkey)
            if w is not None and w.idx not in deps:
                deps[w.idx] = (w, "waw")
            for rd in self.readers.get(wkey, ()):
                if rd.idx not in deps:
                    deps[rd.idx] = (rd, "war")
        for d, kind in deps.values():
            if d.eng == eng and not d.is_dma and not o.is_dma:
                if eng == "pe":
                    continue
                if kind == "war":
                    continue
            o.deps.append(d)
            d.signal = True
        for wkey in writes:
            self.lastw[wkey] = o
            self.readers[wkey] = []
        for r in reads:
            self.readers.setdefault(r, []).append(o)
        if dma is not None:
            o.sem = dma
            o.signal = True
        self.ops.append(o)
        return o

    def emit(self, stack):
        nc = self.nc
        for e in COMPUTE:
            self.eng_sems[e] = stack.enter_context(nc.semaphore("s_" + e))
        dkeys = []
        seen = set()
        for o in self.ops:
            if o.is_dma and o.sem not in seen:
                seen.add(o.sem)
                dkeys.append(o.sem)
        dsem = {k: stack.enter_context(nc.semaphore("d%d" % i)) for i, k in enumerate(dkeys)}
        cnt = {e: 0 for e in COMPUTE}
        dcnt = {k: 0 for k in dkeys}
        for o in self.ops:
            if o.is_dma:
                dcnt[o.sem] += 16
                o.count = dcnt[o.sem]
                o.sem = dsem[o.sem]
            elif o.signal:
                cnt[o.eng] += 1
                o.count = cnt[o.eng]
                o.sem = self.eng_sems[o.eng]
        block = stack.enter_context(nc.Block())
        by_eng = {}
        for o in self.ops:
            by_eng.setdefault(o.eng, []).append(o)

        def run(engname, eng):
            waited = {}
            for o in by_eng.get(engname, ()):
                need = {}
                for d in o.deps:
                    k = id(d.sem)
                    if need.get(k, (None, 0))[1] < d.count:
                        need[k] = (d.sem, d.count)
                for k, (sem, c) in need.items():
                    if waited.get(k, 0) >= c:
                        continue
                    eng.wait_ge(sem, c)
                    waited[k] = c
                if o.fn is None:
                    continue
                ins = o.fn(eng)
                if o.is_dma:
                    ins.then_inc(o.sem, 16)
                elif o.signal:
                    ins.then_inc(o.sem, 1)

        @block.tensor
        def _(e):
            run("pe", e)

        @block.scalar
        def _(e):
            run("act", e)

        @block.vector
        def _(e):
            run("dve", e)

        @block.gpsimd
        def _(e):
            run("pool", e)

        @block.sync
        def _(e):
            run("sp", e)


def _blocks(Wm):
    K, C = Wm.shape
    nk, M = K // 128, C // 128
    return np.ascontiguousarray(
        Wm.reshape(nk, 128, M, 128).transpose(2, 1, 0, 3).reshape(M * 128, nk * 128))


def _col(v):
    return np.ascontiguousarray(np.asarray(v, np.float32).reshape(-1, 128).T)


class PPLayout:
    def __init__(self):
        self.off = {}
        self.n = 0

    def add(self, name, n):
        self.off[name] = self.n
        self.n += n


def _pp_layout():
    L = PPLayout()
    for l in range(2):
        for nm in ("mix_pre", "mix_post", "ffn_pre", "ffn_post"):
            L.add("%s%d" % (nm, l), 16)
    L.add("ab_conv", 24)
    L.add("ab_pscale", 8)
    L.add("rden15", 60)
    L.add("b_pw1", 32)
    L.add("wst", 32 * 16)
    L.add("b_dw", 16)
    L.add("ln_g", 16)
    L.add("ln_b", 16)
    L.add("b_pw2", 16)
    for l in range(2):
        L.add("ffn_conv%d" % l, 3 * 88)
    L.add("eps", 1)
    L.add("zero", 1)
    L.n = (L.n + 7) // 8 * 8
    return L


PPL = _pp_layout()


def build_nc(n_sub=4):
    nc = bass.Bass("TRN2", target_bir_lowering=False)
    dr = {}

    def din(name, shape):
        dr[name] = nc.dram_tensor(name, list(shape), F32, kind="ExternalInput").ap()
        return dr[name]

    din("xT", [128, KD * WIN])
    din("pp", [128, PPL.n])
    din("ident", [128, 32])
    din("poolw", [128, 4 * 2 * 256])
    din("w_in", [32 * 128, D])
    din("w_out", [16 * 128, D])
    din("pw1", [32 * 128, D])
    din("pw2", [16 * 128, D])
    for l in range(2):
        din("up%d" % l, [88 * 128, D])
        din("dn%d" % l, [16 * 128, DFF])
    outT = nc.dram_tensor("outT", [128, KD * WIN], F32, kind="ExternalOutput").ap()

    with ExitStack() as st:
        sizes = [
            ("h", KD * WIN * 4), ("pp", PPL.n * 4), ("zT", KD * TH * 2), ("R1", KF * TH * 2),
            ("M", KD * TH * 4), ("ring", NSLOT * 4096), ("rstd", TH * 4), ("mu", TH * 4), ("rstdA", TH * 4), ("sqC", 2 * TH * 2),
            ("identb", 64), ("onesb", 256), ("poolw", 4096),
            ("st_ffn", 2 * KF * 2 * 4), ("st_pa", 8 * 2 * 4), ("st_vb", 8 * 15 * 4), ("st_c", 16 * 30 * 2),
        ]
        offs = {}
        o = 0
        for nm, sz in sizes:
            offs[nm] = o
            o += (sz + 31) // 32 * 32
        total = o
        arena = st.enter_context(nc.sbuf_tensor("arena", [128, total // 4], F32))

        def view(off, nbytes, dt):
            a = arena[:, off // 4:(off + nbytes) // 4]
            return a if dt == F32 else a.bitcast(dt)

        hT = view(offs["h"], KD * WIN * 4, F32).rearrange("p (k t) -> p k t", k=KD)
        ppt = view(offs["pp"], PPL.n * 4, F32)
        zT = view(offs["zT"], KD * TH * 2, BF16).rearrange("p (k t) -> p k t", k=KD)
        yF = view(offs["R1"], KF * TH * 2, BF16).rearrange("p (k t) -> p k t", k=KF)
        cR = view(offs["R1"], KD * TH * 4, F32).rearrange("p (k t) -> p k t", k=KD)
        mH = view(offs["M"], KD * TH * 4, F32).rearrange("p (k t) -> p k t", k=KD)
        ring = [view(offs["ring"] + s * 4096, 4096, BF16).rearrange("p (k c) -> p k c", k=16)
                for s in range(NSLOT)]
        rstd = view(offs["rstd"], TH * 4, F32)
        mu = view(offs["mu"], TH * 4, F32)
        rstdA = view(offs["rstdA"], TH * 4, F32)
        sqC = [view(offs["sqC"] + i * TH * 2, TH * 2, BF16) for i in range(2)]
        sqA = [view(offs["mu"] + i * TH * 2, TH * 2, BF16) for i in range(2)]
        identb = view(offs["identb"], 64, BF16)
        onesb = view(offs["onesb"], 256, BF16)
        poolwb = view(offs["poolw"], 4096, BF16).rearrange("p (g i o) -> p g i o", g=4, i=2)
        st_ffn = view(offs["st_ffn"], 2 * KF * 2 * 4, F32).rearrange("p (a k t) -> p a k t", a=2, k=KF)
        st_pa = view(offs["st_pa"], 8 * 2 * 4, F32).rearrange("p (k t) -> p k t", k=8)
        st_vb = view(offs["st_vb"], 8 * 15 * 4, F32).rearrange("p (k t) -> p k t", k=8)
        st_c = view(offs["st_c"], 16 * 30 * 2, BF16).rearrange("p (k t) -> p k t", k=16)

        pst = [st.enter_context(nc.psum_tensor("ps%d" % i, [128, 2, 512], F32)) for i in range(4)]
        PROJ_T = (0, 1)
        S1, S2 = 2, 3

        def pcol(name, i=0):
            c = PPL.off[name] + i
            return ppt[:, c:c + 1]

        def v2(ap):
            return ap.rearrange("p (n t) -> p n t", n=2)

        def psv(i):
            return pst[i][:, :, 0:NT]

        MCH = TH * 4

        class Scr:
            reg = []

            def __init__(self, off, size):
                self.off, self.size = off, size
                self.key = ("sc", off)
                c0, c1 = off // MCH, (off + size - 1) // MCH
                self.mkeys = [("M", c) for c in range(c0, c1 + 1)]
                if not any(t.off == off and t.size == size for t in Scr.reg):
                    Scr.reg.append(self)

            def f32(self, n, o=0):
                return view(offs["M"] + self.off + o * 4, n * 4, F32)

            def bf(self, n, o=0):
                return view(offs["M"] + self.off + o * 2, n * 2, BF16)

        RTAIL = (34 * TH * 2 + 31) // 32 * 32
        assert RTAIL + 4 * 2304 <= KF * TH * 2

        class RScr:
            reg = []

            def __init__(self, off, size, ylo=34, extra=()):
                self.off, self.size = off, size
                self.key = ("rs", off, size)
                self.extra = list(extra)
                self.mkeys = [("y", k) for k in range(ylo, KF)]
                for t2 in RScr.reg:
                    if t2.off < off + size and t2.off + t2.size > off and t2.key != self.key:
                        self.mkeys += [t2.key] + t2.extra
                        t2.mkeys += [self.key] + self.extra
                if not any(t2.key == self.key for t2 in RScr.reg):
                    RScr.reg.append(self)

            def f32(self, n, o=0):
                return view(offs["R1"] + self.off + o * 4, n * 4, F32)

            def bf(self, n, o=0):
                return view(offs["R1"] + self.off + o * 2, n * 2, BF16)

        def gen_tiles(base, n, size=2304):
            return [Scr(base + i * size, size) for i in range(n)]

        def program(P, dry):
            wplan = program.wplan
            state = {"wi": 0, "ps": 0, "fresh": set(), "pre_done": False, "nxt": None, "tail": None,
                     "ptiles": PROJ_T}

            def issue(bi, extra_reads=()):
                if bi >= len(wplan):
                    return
                src, nk = wplan[bi]
                slot = bi % NSLOT
                P.op("pool", lambda e: e.dma_start(out=ring[slot][:, 0:nk, :], in_=src(),
                                                    max_dma_last_dim=8192),
                     reads=list(extra_reads), writes=[("w", slot)], dma=("w", slot))

            def load_x(half, extra_reads):
                xv = dr["xT"].rearrange("p (k t) -> p k t", k=KD)
                for kk in range(4):
                    P.op("sp", lambda e, half=half, kk=kk: e.dma_start(
                        out=hT[:, kk * 4:(kk + 1) * 4, half * TH:(half + 1) * TH],
                        in_=xv[:, kk * 4:(kk + 1) * 4, half * TH:(half + 1) * TH]),
                        reads=list(extra_reads),
                        writes=[("h", k, half) for k in range(kk * 4, kk * 4 + 4)], dma=("x", half, kk))

            def wget(src, nk):
                bi = state["wi"]
                state["wi"] += 1
                if dry:
                    wplan.append((src, nk))
                    return 0
                if bi == 0:
                    hk0 = [("h", k, 0) for k in range(KD)]
                    for j in range(NSLOT):
                        issue(j, hk0)
                    load_x(1, [("w", sl) for sl in range(NSLOT)])
                else:
                    issue(bi + NSLOT - 1)
                return bi % NSLOT

            def wsrc(name, m, k0, nk):
                return lambda: dr[name][m * 128:(m + 1) * 128, k0 * 128:(k0 + nk) * 128].rearrange(
                    "p (k c) -> p k c", c=128)

            def psnext():
                tl = state["ptiles"]
                i = tl[state["ps"] % len(tl)]
                state["ps"] += 1
                return i

            def drain_tail(chunks):
                tl = state["tail"]
                if tl is None:
                    return
                if chunks == KD:
                    chunks = range(KD)
                for c in sorted(set(chunks) & tl[1]):
                    tl[0](c)
                    tl[1].discard(c)
                if not tl[1]:
                    state["tail"] = None

            def scw(t):
                if t.key in state["fresh"]:
                    return [t.key]
                state["fresh"].add(t.key)
                drain_tail([mk[1] for mk in t.mkeys])
                return [t.key] + t.mkeys

            def proj(parts, rhs, rkey, first_extra=()):
                pt = psnext()
                ktot = sum(p[3] for p in parts)
                kk = 0
                for (name, m, k0, nk) in parts:
                    slot = wget(wsrc(name, m, k0, nk), nk)
                    for k in range(nk):
                        for nt, (o, n) in enumerate(NTS):
                            P.op("pe", (lambda e, slot=slot, k=k, nt=nt, o=o, n=n, kk=kk, pt=pt, kg=k0 + k:
                                        e.matmul(pst[pt][:, nt, 0:n], lhsT=ring[slot][:, k, :],
                                                 rhs=rhs(kg)[:, o:o + n],
                                                 start=(kk == 0), stop=(kk == ktot - 1))),
                                 reads=[("w", slot), rkey(k0 + k)] + (list(first_extra) if kk == 0 else []),
                                 writes=[("ps", pt)])
                        kk += 1
                return pt

            def ones_mm(tile, src_ap, skey, first, last):
                for nt, (o, n) in enumerate(NTS):
                    P.op("pe", (lambda e, nt=nt, o=o, n=n:
                                e.matmul(pst[tile][:, nt, 0:n], lhsT=onesb, rhs=src_ap[:, o:o + n],
                                         start=first, stop=last)),
                         reads=[skey, "onesb"], writes=[("ps", tile)])

            def rstd_from(tile, dst, dkey):
                P.op("act", lambda e: e.activation(out=v2(dst), in_=psv(tile), func=AF.Ln,
                                                   scale=1.0 / D, bias=pcol("eps")),
                     reads=[("ps", tile), "pp"], writes=[dkey])
                P.op("act", lambda e: e.activation(out=dst, in_=dst, func=AF.Exp, scale=-0.5),
                     reads=[dkey], writes=[dkey])

            def A_sq(k, half):
                t0 = half * TH
                i = k % 2
                wk = [("sqA", i)] + (["mu"] if k < 2 else [])
                P.op("act", lambda e, k=k, i=i: e.activation(out=sqA[i], in_=hT[:, k, t0:t0 + TH],
                                                              func=AF.Square),
                     reads=[("h", k, half)], writes=wk)

            def A_add(p):
                P.op("dve", lambda e: e.tensor_tensor(out=sqA[0], in0=sqA[0], in1=sqA[1], op=ALU.add),
                     reads=[("sqA", 0), ("sqA", 1)], writes=[("sqA", 0)])

            def A_pmm(p):
                ones_mm(S2, sqA[0], ("sqA", 0), p == 0, p == KD // 2 - 1)

            def A_pair(p):
                A_add(p)
                A_pmm(p)

            def A_fin():
                rstd_from(S2, rstdA, "rstdA")

            def A_z(k, half, gname):
                t0 = half * TH
                P.op("dve", lambda e, k=k: e.scalar_tensor_tensor(
                    out=zT[:, k, :], in0=hT[:, k, t0:t0 + TH], scalar=pcol(gname, k), in1=rstdA,
                    op0=ALU.mult, op1=ALU.mult),
                    reads=[("h", k, half), "rstdA", "pp"], writes=[("zT", k)])

            def phaseA(half, gname):
                if state["pre_done"]:
                    state["pre_done"] = False
                    return
                for k in range(KD):
                    A_sq(k, half)
                    ones_mm(S2, sqA[k % 2], ("sqA", k % 2), k == 0, k == KD - 1)
                A_fin()
                for k in range(KD):
                    A_z(k, half, gname)

            def phaseC(half, wname, nkc, rhs, rkey, gname, bias_name, sq_tiles, overlap=True, first_extra=()):
                t0 = half * TH
                parts_of = lambda m: [(wname, m, k0, min(16, nkc - k0)) for k0 in range(0, nkc, 16)]
                pend = None
                state["ptiles"] = PROJ_T
                nxt = state["nxt"] if overlap else None
                for m in range(KD):
                    if m == 0:
                        drain_tail(KD)
                    if nxt is not None:
                        nh, ng = nxt
                        if 1 <= m <= 8:
                            A_pmm(m - 1)
                        if m < 8:
                            A_sq(2 * m, nh)
                            A_sq(2 * m + 1, nh)
                            A_add(m)
                        else:
                            if m == 8:
                                A_fin()
                            A_z(2 * (m - 8), nh, ng)
                            A_z(2 * (m - 8) + 1, nh, ng)
                    pt = proj(parts_of(m), rhs, rkey, first_extra if m == 0 else ())
                    if pend is not None:
                        ones_mm(S1, pend[0], pend[1], pend[2] == 0, False)
                        pend = None
                    b = pcol(bias_name, m) if bias_name else 0.0
                    alias = [t.key for t in Scr.reg
                             if t.off < (m + 1) * MCH and t.off + t.size > m * MCH]
                    P.op("act", lambda e, m=m, pt=pt, b=b: e.activation(
                        out=v2(mH[:, m, :]), in_=psv(pt), func=AF.Identity, bias=b),
                        reads=[("ps", pt), "pp"], writes=[("M", m)] + alias)
                    sq_ap, sq_key = sq_tiles[m % 2]
                    P.op("act", lambda e, pt=pt, b=b, sq_ap=sq_ap: e.activation(
                        out=v2(sq_ap), in_=psv(pt), func=AF.Square, bias=b),
                        reads=[("ps", pt), "pp"], writes=[sq_key])
                    if m % 2 == 1 and m < KD - 1:
                        (s0, k0_), (s1, k1_) = sq_tiles[0], sq_tiles[1]
                        P.op("dve", lambda e, s0=s0, s1=s1: e.tensor_tensor(out=s0, in0=s0, in1=s1, op=ALU.add),
                             reads=[k0_, k1_], writes=[k0_])
                        pend = (s0, k0_, m // 2)
                    elif m == KD - 2:
                        pend = (sq_ap, sq_key, m // 2)
                    elif m == KD - 1:
                        pend = (sq_ap, sq_key, m // 2)
                ones_mm(S1, pend[0], pend[1], False, True)
                if nxt is not None:
                    state["pre_done"] = True
                rstd_from(S1, rstd, "rstd")

                def tail(m):
                    P.op("dve", lambda e, m=m: e.scalar_tensor_tensor(
                        out=mH[:, m, :], in0=mH[:, m, :], scalar=pcol(gname, m), in1=rstd,
                        op0=ALU.mult, op1=ALU.mult),
                        reads=[("M", m), "rstd", "pp"], writes=[("M", m)])
                    P.op("dve", lambda e, m=m: e.tensor_tensor(
                        out=hT[:, m, t0:t0 + TH], in0=hT[:, m, t0:t0 + TH], in1=mH[:, m, :], op=ALU.add),
                        reads=[("M", m), ("h", m, half)], writes=[("h", m, half)])
                state["tail"] = [tail, set(range(KD))]

            zsq = [(sqC[i], ("sqC", i)) for i in range(2)]

            def ffn(l, half):
                state["fresh"] = set()
                phaseA(half, "ffn_pre%d" % l)
                state["ptiles"] = (0, 1, S2)
                tilesB = gen_tiles(0, 4)
                tilesAM = gen_tiles(4 * 2304, 4)
                tilesAR = [RScr(RTAIL + j * 2304, 2304) for j in range(4)]
                up = "up%d" % l
                cw = "ffn_conv%d" % l
                for i in range(KF):
                    r = i % 2
                    ug, tg, uv, tv = tilesB if r == 1 else (tilesAR if i < 30 else tilesAM)
                    if 8 <= i < 20:
                        drain_tail([i - 4])
                    res = []
                    for which, (u_t, t_t, blk) in enumerate(((ug, tg, i), (uv, tv, KF + i))):
                        pt = proj([(up, blk, 0, 16)], lambda k: zT[:, k, :], lambda k: ("zT", k))
                        u = u_t.f32(TH + 2)
                        t = t_t.f32(TH)
                        if half == 0:
                            P.op("pool", lambda e, u=u: e.memset(u[:, 0:2], 0.0), writes=scw(u_t))
                        else:
                            P.op("pool", lambda e, u=u, which=which, i=i: e.tensor_copy(
                                out=u[:, 0:2], in_=st_ffn[:, which, i, :]),
                                reads=[("st_ffn", which, i)], writes=scw(u_t))
                        P.op("act", lambda e, u=u, pt=pt: e.activation(
                            out=v2(u[:, 2:TH + 2]), in_=psv(pt), func=AF.Copy),
                            reads=[("ps", pt)], writes=[u_t.key])
                        P.op("act", lambda e, t=t, pt=pt, blk=blk: e.activation(
                            out=v2(t), in_=psv(pt), func=AF.Copy, scale=pcol(cw, 2 * 88 + blk)),
                            reads=[("ps", pt), "pp"], writes=scw(t_t))
                        if half == 0:
                            P.op("pool", lambda e, u=u, which=which, i=i: e.tensor_copy(
                                out=st_ffn[:, which, i, :], in_=u[:, TH:TH + 2]),
                                reads=[u_t.key], writes=[("st_ffn", which, i)])
                        P.op("dve", lambda e, u=u, t=t, blk=blk: e.scalar_tensor_tensor(
                            out=t, in0=u[:, 1:TH + 1], scalar=pcol(cw, 88 + blk), in1=t,
                            op0=ALU.mult, op1=ALU.add),
                            reads=[u_t.key, t_t.key, "pp"], writes=[t_t.key])
                        P.op("dve", lambda e, u=u, t=t, blk=blk: e.scalar_tensor_tensor(
                            out=t, in0=u[:, 0:TH], scalar=pcol(cw, blk), in1=t,
                            op0=ALU.mult, op1=ALU.add),
                            reads=[u_t.key, t_t.key, "pp"], writes=[t_t.key])
                        res.append(t)
                    P.op("act", lambda e, tgv=res[0]: e.activation(out=tgv, in_=tgv, func=AF.Silu),
                         reads=[tg.key], writes=[tg.key])
                    P.op("dve", lambda e, tgv=res[0], tvv=res[1], i=i: e.tensor_tensor(
                        out=yF[:, i, :], in0=tgv, in1=tvv, op=ALU.mult),
                        reads=[tg.key, tv.key],
                        writes=[("y", i)] + [kk_ for t in RScr.reg
                                             if t.off < (i + 1) * TH * 2 and t.off + t.size > i * TH * 2
                                             for kk_ in [t.key] + t.extra])
                phaseC(half, "dn%d" % l, KF, lambda k: yF[:, k, :], lambda k: ("y", k),
                       "ffn_post%d" % l, None, zsq)

            def mixer0(half):
                state["fresh"] = set()
                phaseA(half, "mix_pre0")
                state["ptiles"] = (0, 1, S2)
                RM0 = (KD * TH * 2 + 31) // 32 * 32
                assert RM0 + 12 * 2304 <= KF * TH * 2
                tiles = [RScr(RM0 + j_ * 2304, 2304, KD) for j_ in range(12)]
                zr = lambda k: zT[:, k, :]
                zk = lambda k: ("zT", k)
                for j in range(8):
                    r = j % 2
                    drain_tail([j])
                    gc_t, pa_t, t_t = tiles[r * 3:(r + 1) * 3]
                    gc = gc_t.f32(TH)
                    pa = pa_t.f32(TH + 2)
                    t = t_t.f32(TH)
                    p_gc = proj([("w_in", 8 + j, 0, 16)], zr, zk)
                    P.op("act", lambda e, gc=gc, p=p_gc: e.activation(out=v2(gc), in_=psv(p), func=AF.Copy),
                         reads=[("ps", p_gc)], writes=scw(gc_t))
                    p_va = proj([("w_in", 16 + j, 0, 16)], zr, zk)
                    if half == 0:
                        P.op("pool", lambda e, pa=pa: e.memset(pa[:, 0:2], 0.0), writes=scw(pa_t))
                    else:
                        P.op("pool", lambda e, pa=pa, j=j: e.tensor_copy(out=pa[:, 0:2], in_=st_pa[:, j, :]),
                             reads=[("st_pa", j)], writes=scw(pa_t))
                    P.op("dve", lambda e, pa=pa, gc=gc, p=p_va: e.tensor_tensor(
                        out=v2(pa[:, 2:TH + 2]), in0=v2(gc), in1=psv(p), op=ALU.mult),
                        reads=[gc_t.key, ("ps", p_va)], writes=[pa_t.key])
                    if half == 0:
                        P.op("pool", lambda e, pa=pa, j=j: e.tensor_copy(out=st_pa[:, j, :], in_=pa[:, TH:TH + 2]),
                             reads=[pa_t.key], writes=[("st_pa", j)])
                    P.op("dve", lambda e, pa=pa, t=t, j=j: e.tensor_scalar(
                        out=t, in0=pa[:, 2:TH + 2], scalar1=pcol("ab_conv", 16 + j), scalar2=None, op0=ALU.mult),
                        reads=[pa_t.key, "pp"], writes=scw(t_t))
                    P.op("dve", lambda e, pa=pa, t=t, j=j: e.scalar_tensor_tensor(
                        out=t, in0=pa[:, 1:TH + 1], scalar=pcol("ab_conv", 8 + j), in1=t,
                        op0=ALU.mult, op1=ALU.add), reads=[pa_t.key, t_t.key, "pp"], writes=[t_t.key])
                    P.op("dve", lambda e, pa=pa, t=t, j=j: e.scalar_tensor_tensor(
                        out=t, in0=pa[:, 0:TH], scalar=pcol("ab_conv", j), in1=t,
                        op0=ALU.mult, op1=ALU.add), reads=[pa_t.key, t_t.key, "pp"], writes=[t_t.key])
                    p_gb = proj([("w_in", j, 0, 16)], zr, zk)
                    P.op("dve", lambda e, t=t, p=p_gb, j=j: e.tensor_tensor(
                        out=v2(yF[:, j, :]), in0=v2(t), in1=psv(p), op=ALU.mult),
                        reads=[t_t.key, ("ps", p_gb)], writes=[("y", j)])
                HL = 15
                EL = TH + HL
                pgbase = 12 * 2304
                pgt = [[Scr(pgbase + (2 * s_ + ic) * 1088, 1088) for ic in range(2)] for s_ in range(2)]
                tmp_t = Scr(pgbase + 4 * 1088, 64)

                def pool_mm(g):
                    pgs = [pgt[g % 2][ic].bf(TH) for ic in range(2)]
                    for oc in range(2):
                        pt = psnext()
                        for ic in range(2):
                            for nt, (o, n) in enumerate(NTS):
                                P.op("pe", (lambda e, pt=pt, nt=nt, o=o, n=n, ic=ic, oc=oc, g=g, pgs=pgs:
                                            e.matmul(pst[pt][:, nt, 0:n],
                                                     lhsT=poolwb[:, g, ic, oc * 128:(oc + 1) * 128],
                                                     rhs=pgs[ic][:, o:o + n], start=(ic == 0), stop=(ic == 1))),
                                     reads=[pgt[g % 2][ic].key, "poolw"], writes=[("ps", pt)])
                        P.op("act", lambda e, pt=pt, g=g, oc=oc: e.activation(
                            out=v2(yF[:, 8 + 2 * g + oc, :]), in_=psv(pt), func=AF.Copy,
                            scale=pcol("ab_pscale", 2 * g + oc)),
                            reads=[("ps", pt), "pp"], writes=[("y", 8 + 2 * g + oc)])
                for c in range(8):
                    g = c // 2
                    drain_tail([8 + c])
                    wdw = 2 << g
                    r = c % 2
                    e_t, sa_t, sb_t = tiles[6 + r * 3:6 + (r + 1) * 3]
                    pg_t = pgt[g % 2][c % 2]
                    ee = e_t.f32(EL)
                    sa = sa_t.f32(EL)
                    sb = sb_t.f32(EL)
                    p_vb = proj([("w_in", 24 + c, 0, 16)], zr, zk)
                    if half == 0:
                        P.op("pool", lambda e, ee=ee: e.memset(ee[:, 0:HL], 0.0), writes=scw(e_t))
                    else:
                        P.op("pool", lambda e, ee=ee, c=c: e.tensor_copy(out=ee[:, 0:HL], in_=st_vb[:, c, :]),
                             reads=[("st_vb", c)], writes=scw(e_t))
                    P.op("act", lambda e, ee=ee, p=p_vb: e.activation(
                        out=v2(ee[:, HL:EL]), in_=psv(p), func=AF.Copy),
                        reads=[("ps", p_vb)], writes=[e_t.key])
                    if half == 0:
                        P.op("pool", lambda e, ee=ee, c=c: e.tensor_copy(out=st_vb[:, c, :], in_=ee[:, TH:EL]),
                             reads=[e_t.key], writes=[("st_vb", c)])
                    if c % 2 == 1 and g >= 1:
                        pool_mm(g - 1)
                    src, src_t = ee, e_t
                    sh = 1
                    lo = 0
                    bufs = [(sa, sa_t), (sb, sb_t)]
                    bi = 0
                    while sh < wdw:
                        dst, dst_t = bufs[bi]
                        bi ^= 1
                        lo2 = lo + sh
                        P.op("dve", lambda e, dst=dst, src=src, lo2=lo2, sh=sh: e.tensor_tensor(
                            out=dst[:, lo2:EL], in0=src[:, lo2:EL], in1=src[:, lo2 - sh:EL - sh], op=ALU.add),
                            reads=[src_t.key], writes=scw(dst_t))
                        src, src_t = dst, dst_t
                        lo = lo2
                        sh *= 2
                    pgb = pg_t.bf(TH)
                    P.op("dve", lambda e, pgb=pgb, src=src, ee=ee, wdw=wdw: e.scalar_tensor_tensor(
                        out=pgb, in0=src[:, HL:EL], scalar=1.0 / wdw, in1=ee[:, HL:EL],
                        op0=ALU.mult, op1=ALU.subtract),
                        reads=[src_t.key, e_t.key], writes=scw(pg_t))
                    if half == 0:
                        tmp = tmp_t.f32(HL)
                        P.op("dve", lambda e, tmp=tmp, src=src, g=g: e.tensor_tensor(
                            out=tmp, in0=src[:, HL:2 * HL],
                            in1=ppt[:, PPL.off["rden15"] + g * 15:PPL.off["rden15"] + (g + 1) * 15], op=ALU.mult),
                            reads=[src_t.key, "pp"], writes=scw(tmp_t))
                        P.op("dve", lambda e, tmp=tmp, pgb=pgb, ee=ee: e.tensor_tensor(
                            out=pgb[:, 0:HL], in0=tmp, in1=ee[:, HL:2 * HL], op=ALU.subtract),
                            reads=[tmp_t.key, pg_t.key, e_t.key], writes=[pg_t.key])
                pool_mm(3)
                phaseC(half, "w_out", 16, lambda k: yF[:, k, :], lambda k: ("y", k), "mix_post0", None, zsq)

            def mixer1(half):
                state["fresh"] = set()
                phaseA(half, "mix_pre1")
                state["ptiles"] = PROJ_T
                yC = lambda k: cR[:, k, 0:NT].bitcast(BF16)
                XL = TH + 28
                LB = 32 * 32 * 2
                XB = 4 * XL * 2
                tiles = gen_tiles(0, 8)
                TB = 8 * 2304
                RB = KD * TH * 4
                assert RB + 2 * XB + LB <= KF * TH * 2
                Xt = [RScr(RB + r_ * XB, XB, 32, [("X", r_, g, j) for g in range(4) for j in range(4)])
                      for r_ in range(2)]
                Lt = [RScr(RB + 2 * XB, LB, 32), Scr(TB, LB)]
                zr = lambda k: zT[:, k, :]
                zk = lambda k: ("zT", k)
                CL = TH + 30
                pend1 = None
                pend_conv = None
                for i in range(KD):
                    r = i % 2
                    sg_t, cin_t, cs_t = tiles[r * 3:(r + 1) * 3]
                    sg = sg_t.f32(TH)
                    cin = cin_t.bf(CL)
                    cb = cs_t.bf(TH)
                    sqb = cs_t.bf(TH, o=TH)
                    L_t, X_t = Lt[r], Xt[r]
                    Lb = L_t.bf(32 * 32).rearrange("p (a c) -> p a c", a=32)
                    Xb = X_t.bf(4 * XL).rearrange("p (g t) -> p g t", g=4)
                    xkeys = [("X", r, g, j) for g in range(4) for j in range(4)]
                    p_g = proj([("pw1", 16 + i, 0, 16)], zr, zk)
                    P.op("act", lambda e, sg=sg, p=p_g, i=i: e.activation(
                        out=v2(sg), in_=psv(p), func=AF.Sigmoid, bias=pcol("b_pw1", 16 + i)),
                        reads=[("ps", p_g), "pp"], writes=scw(sg_t))
                    p_a = proj([("pw1", i, 0, 16)], zr, zk)
                    if half == 0:
                        P.op("pool", lambda e, cin=cin: e.memset(cin[:, 0:30], 0.0), writes=scw(cin_t))
                    else:
                        P.op("pool", lambda e, cin=cin, i=i: e.tensor_copy(out=cin[:, 0:30], in_=st_c[:, i, :]),
                             reads=[("st_c", i)], writes=scw(cin_t))
                    P.op("dve", lambda e, cin=cin, sg=sg, p=p_a, i=i: e.scalar_tensor_tensor(
                        out=v2(cin[:, 30:CL]), in0=psv(p), scalar=pcol("b_pw1", i), in1=v2(sg),
                        op0=ALU.add, op1=ALU.mult),
                        reads=[("ps", p_a), sg_t.key, "pp"], writes=[cin_t.key])
                    if half == 0:
                        P.op("pool", lambda e, cin=cin, i=i: e.tensor_copy(out=st_c[:, i, :], in_=cin[:, TH:CL]),
                             reads=[cin_t.key], writes=[("st_c", i)])
                    wd = ppt[:, PPL.off["wst"] + i * 32:PPL.off["wst"] + (i + 1) * 32]
                    P.op("dve", lambda e, Lb=Lb, wd=wd: e.tensor_tensor(
                        out=Lb, in0=identb[:, None, :].broadcast_to([128, 32, 32]),
                        in1=wd[:, :, None].broadcast_to([128, 32, 32]), op=ALU.mult),
                        reads=["identb", "pp"], writes=scw(L_t))
                    P.op("pool", lambda e, Xb=Xb: e.memset(Xb[:, :, XL - 1:XL], 0.0),
                         writes=scw(X_t) + xkeys)
                    for g in range(4):
                        q = "sp" if g < 2 else "pool"
                        for j in range(4):
                            ncol = XL if j < 3 else XL - 1
                            P.op(q, lambda e, g=g, j=j, ncol=ncol, Xb=Xb, cin=cin: e.dma_start(
                                out=Xb[32 * j:32 * j + 32, g, 0:ncol], in_=cin[32 * g:32 * g + 32, j:j + ncol]),
                                reads=[cin_t.key], writes=[("X", r, g, j)], dma=("xs", r, q))

                    def conv_evac(i=i, r=r, Lb=Lb, Xb=Xb, L_t=L_t, xkeys=xkeys, cs_t=cs_t, cb=cb, sqb=sqb):
                        pt = psnext()
                        for b_ in range(8):
                            for nt, (o, n) in enumerate(NTS):
                                for g in range(4):
                                    P.op("pe", (lambda e, pt=pt, nt=nt, o=o, n=n, b_=b_, g=g:
                                                e.matmul(pst[pt][32 * g:32 * g + 32, nt, 0:n],
                                                         lhsT=Lb[:, g * 8 + b_, :],
                                                         rhs=Xb[:, g, o + 4 * b_:o + 4 * b_ + n],
                                                         start=(b_ == 0), stop=(b_ == 7),
                                                         tile_position=(0, 32 * g))),
                                         reads=[L_t.key] + xkeys, writes=[("ps", pt)])
                        P.op("act", lambda e, pt=pt: e.activation(
                            out=v2(cR[:, i, :]), in_=psv(pt), func=AF.Identity, bias=pcol("b_dw", i)),
                            reads=[("ps", pt), "pp"], writes=[("c", i)] + ([("csq", i)] if i < 4 else []))
                        P.op("act", lambda e, pt=pt: e.activation(
                            out=v2(cb), in_=psv(pt), func=AF.Identity, bias=pcol("b_dw", i)),
                            reads=[("ps", pt), "pp"], writes=scw(cs_t))
                        P.op("act", lambda e, pt=pt: e.activation(
                            out=v2(sqb), in_=psv(pt), func=AF.Square, bias=pcol("b_dw", i)),
                            reads=[("ps", pt), "pp"], writes=[cs_t.key])
                        return (cb, sqb, cs_t.key, i)

                    if pend_conv is not None:
                        if pend1 is not None:
                            ones_mm(S1, pend1[0], pend1[2], pend1[3] == 0, False)
                            ones_mm(S2, pend1[1], pend1[2], pend1[3] == 0, False)
                        pend1 = pend_conv()
                    pend_conv = conv_evac
                ones_mm(S1, pend1[0], pend1[2], False, False)
                ones_mm(S2, pend1[1], pend1[2], False, False)
                pend1 = pend_conv()
                ones_mm(S1, pend1[0], pend1[2], False, True)
                ones_mm(S2, pend1[1], pend1[2], False, True)
                drain_tail(KD)
                t6, t7 = tiles[6], tiles[7]
                ex2 = t6.f32(TH)
                msq = t7.f32(TH)
                P.op("act", lambda e: e.activation(out=v2(msq), in_=psv(S1), func=AF.Square, scale=1.0 / D),
                     reads=[("ps", S1)], writes=scw(t7))
                P.op("act", lambda e: e.activation(out=v2(mu), in_=psv(S1), func=AF.Copy, scale=1.0 / D),
                     reads=[("ps", S1)], writes=["mu", ("sqA", 0), ("sqA", 1)])
                P.op("dve", lambda e: e.scalar_tensor_tensor(out=v2(ex2), in0=psv(S2), scalar=1.0 / D, in1=v2(msq),
                                                             op0=ALU.mult, op1=ALU.subtract),
                     reads=[("ps", S2), t7.key], writes=scw(t6))
                P.op("act", lambda e: e.activation(out=rstd, in_=ex2, func=AF.Ln, scale=1.0, bias=pcol("eps")),
                     reads=[t6.key, "pp"], writes=["rstd"])
                P.op("act", lambda e: e.activation(out=rstd, in_=rstd, func=AF.Exp, scale=-0.5),
                     reads=["rstd"], writes=["rstd"])
                for i in range(KD):
                    lt_t = tiles[i % 2 * 3]
                    lt = lt_t.f32(TH)
                    P.op("dve", lambda e, lt=lt, i=i: e.tensor_tensor(out=lt, in0=cR[:, i, :], in1=mu, op=ALU.subtract),
                         reads=[("c", i), "mu"], writes=scw(lt_t))
                    P.op("dve", lambda e, lt=lt: e.tensor_tensor(out=lt, in0=lt, in1=rstd, op=ALU.mult),
                         reads=[lt_t.key, "rstd"], writes=[lt_t.key])
                    P.op("act", lambda e, lt=lt, i=i: e.activation(
                        out=yC(i), in_=lt, func=AF.Silu, scale=pcol("ln_g", i), bias=pcol("ln_b", i)),
                        reads=[lt_t.key, "pp"], writes=[("c", i)])
                csq = [(cR[:, j, NT:TH].bitcast(BF16), ("csq", j)) for j in range(4)]
                phaseC(half, "pw2", 16, yC, lambda k: ("c", k), "mix_post1", "b_pw2", csq,
                       first_extra=[("c", KD - 8)])

            P.op("sp", lambda e: e.dma_start(out=ppt, in_=dr["pp"]), writes=["pp"], dma="ld_pp")
            P.op("pool", lambda e: e.dma_start(out=identb, in_=dr["ident"]), writes=["identb"], dma="ld_id")
            P.op("pool", lambda e: e.dma_start(out=poolwb, in_=dr["poolw"].rearrange(
                "p (g i o) -> p g i o", g=4, i=2)), writes=["poolw"], dma="ld_pw")
            P.op("dve", lambda e: e.memset(onesb, 1.0), writes=["onesb"])
            xv = dr["xT"].rearrange("p (k t) -> p k t", k=KD)
            ov = outT.rearrange("p (k t) -> p k t", k=KD)
            load_x(0, ())
            subs = [lambda h: mixer0(h), lambda h: ffn(0, h), lambda h: mixer1(h), lambda h: ffn(1, h)]
            pre_g = ["mix_pre0", "ffn_pre0", "mix_pre1", "ffn_pre1"]
            steps = [(si, half) for si in range(n_sub) for half in range(2)]
            state["pre_done"] = False
            for n, (si, half) in enumerate(steps):
                if n + 1 < len(steps):
                    state["nxt"] = (steps[n + 1][1], pre_g[steps[n + 1][0]])
                else:
                    state["nxt"] = None
                subs[si](half)
            drain_tail(KD)
            okeys = []
            for half in range(2):
                for kk in range(4):
                    P.op("sp", lambda e, half=half, kk=kk: e.dma_start(
                        out=ov[:, kk * 4:(kk + 1) * 4, half * TH:(half + 1) * TH],
                        in_=hT[:, kk * 4:(kk + 1) * 4, half * TH:(half + 1) * TH]),
                        reads=[("h", k, half) for k in range(kk * 4, kk * 4 + 4)],
                        writes=[("out", half, kk)], dma=("o", half, kk))
                    okeys.append(("out", half, kk))
            P.op("sp", None, reads=okeys)

        program.wplan = []
        program(Prog(nc, dry=True), True)
        P = Prog(nc)
        program(P, False)
        P.emit(st)
    return nc


def _prep_shared(inp):
    f = lambda a: np.asarray(a, np.float32)
    sh = {}
    sh["w_in"] = _blocks(f(inp["ab_w_in"])[0])
    sh["w_out"] = _blocks(f(inp["ab_w_out"])[0])
    sh["pw1"] = _blocks(f(inp["c_w_pw1"])[0])
    sh["pw2"] = _blocks(f(inp["c_w_pw2"])[0])
    for l in range(2):
        sh["up%d" % l] = _blocks(f(inp["ffn_w_up"])[l])
        sh["dn%d" % l] = _blocks(f(inp["ffn_w_down"])[l])
    pw = f(inp["ab_pool_w"])[0]
    sh["poolw"] = np.ascontiguousarray(
        pw.reshape(4, 2, 128, 256).transpose(2, 0, 1, 3).reshape(128, 4 * 2 * 256))
    sh["ident"] = np.ascontiguousarray(np.tile(np.eye(32, dtype=np.float32), (4, 1)))
    pp = np.zeros((128, PPL.n), np.float32)

    def put(name, arr):
        pp[:, PPL.off[name]:PPL.off[name] + arr.shape[1]] = arr

    for l in range(2):
        put("mix_pre%d" % l, _col(f(inp["mix_pre_g"])[l]))
        put("mix_post%d" % l, _col(f(inp["mix_post_g"])[l]))
        put("ffn_pre%d" % l, _col(f(inp["ffn_pre_g"])[l]))
        put("ffn_post%d" % l, _col(f(inp["ffn_post_g"])[l]))
        cw = f(inp["ffn_conv_w"])[l]
        put("ffn_conv%d" % l, np.concatenate([_col(cw[t]) for t in range(3)], axis=1))
    acw = f(inp["ab_conv_w"])[0]
    put("ab_conv", np.concatenate([_col(acw[t]) for t in range(3)], axis=1))
    put("ab_pscale", _col(f(inp["ab_pool_scale"])[0]))
    rd = np.zeros((128, 60), np.float32)
    for g, w in enumerate((2, 4, 8, 16)):
        for p in range(15):
            rd[:, g * 15 + p] = np.float32(1.0) / np.float32(min(p + 1, w))
    put("rden15", rd)
    put("b_pw1", _col(f(inp["c_b_pw1"])[0]))
    wdw = f(inp["c_w_dw"])[0]
    w32 = np.concatenate([wdw, np.zeros((1, D), np.float32)], axis=0)
    wst = w32.reshape(8, 4, 16, 4, 32).transpose(1, 4, 2, 3, 0).reshape(128, 16 * 32)
    put("wst", np.ascontiguousarray(wst))
    put("b_dw", _col(f(inp["c_b_dw"])[0]))
    put("ln_g", _col(f(inp["c_ln_g"])[0]))
    put("ln_b", _col(f(inp["c_ln_b"])[0]))
    put("b_pw2", _col(f(inp["c_b_pw2"])[0]))
    pp[:, PPL.off["eps"]] = EPS
    sh["pp"] = pp
    return sh


def _windows(x, meta):
    B = x.shape[0]
    wins, offs = [], []
    for b in range(B):
        seq = np.concatenate([meta, x[b]], axis=0)
        for c in range(4):
            if c == 0:
                s, ooff = 0, NMETA
            else:
                s, ooff = NMETA + CH * c - HALO, HALO
            w = seq[s:s + WIN]
            wt = np.ascontiguousarray(w.T.reshape(KD, 128, WIN).transpose(1, 0, 2).reshape(128, KD * WIN))
            wins.append(wt)
            offs.append(ooff)
    return wins, offs


_NC_CACHE = {}


def kernel(**inputs):
    x = np.asarray(inputs["x"], np.float32)
    meta = np.asarray(inputs["meta_tokens"], np.float32)
    sh = _prep_shared(inputs)
    wins, offs = _windows(x, meta)
    if "nc" not in _NC_CACHE:
        _NC_CACHE["nc"] = build_nc()
    nc = _NC_CACHE["nc"]
    in_maps = []
    for c in range(8):
        m = dict(sh)
        m["xT"] = wins[c]
        in_maps.append(m)
    res = run_bass_kernel_spmd(nc, in_maps, core_ids=list(range(8)))
    out = np.empty((x.shape[0], SEQ, D), np.float32)
    for c in range(8):
        oT = res.results[c]["outT"].reshape(128, KD, WIN)
        tok = oT.transpose(2, 1, 0).reshape(WIN, D)
        b, q = divmod(c, 4)
        out[b, q * CH:(q + 1) * CH] = tok[offs[c]:offs[c] + CH]
    return out
```

```python
import numpy as np
from contextlib import ExitStack
import concourse.bass as bass
import concourse.mybir as mybir
from concourse.bass_utils import run_bass_kernel_spmd

F32 = mybir.dt.float32
BF16 = mybir.dt.bfloat16
ALU = mybir.AluOpType
AF = mybir.ActivationFunctionType

D = 2048
KD = 16
SEQ = 4096
NMETA = 16
HALO = 56
CH = 1024
WIN = CH + HALO
TH = WIN // 2
NT = TH // 2
NTS = ((0, NT), (NT, NT))
DFF = 5632
KF = 44
EPS = 1e-6
NSLOT = 5
COMPUTE = ("pe", "act", "dve", "pool")


class Op:
    __slots__ = ("eng", "fn", "deps", "signal", "count", "sem", "is_dma", "idx")

    def __init__(self, eng, fn, is_dma):
        self.eng = eng
        self.fn = fn
        self.deps = []
        self.signal = False
        self.count = None
        self.sem = None
        self.is_dma = is_dma


class Prog:
    def __init__(self, nc, dry=False):
        self.nc = nc
        self.dry = dry
        self.ops = []
        self.lastw = {}
        self.readers = {}
        self.eng_sems = {}

    def op(self, eng, fn, reads=(), writes=(), dma=None):
        if self.dry:
            return None
        o = Op(eng, fn, dma is not None)
        o.idx = len(self.ops)
        deps = {}
        for r in reads:
            w = self.lastw.get(r)
            if w is not None:
                deps[w.idx] = (w, "raw")
        for wkey in writes:
            w = self.lastw.get(wkey)
            if w is not None and w.idx not in deps:
                deps[w.idx] = (w, "waw")
            for rd in self.readers.get(wkey, ()):
                if rd.idx not in deps:
                    deps[rd.idx] = (rd, "war")
        for d, kind in deps.values():
            if d.eng == eng and not d.is_dma and not o.is_dma:
                if eng == "pe":
                    continue
                if kind == "war":
                    continue
            o.deps.append(d)
            d.signal = True
        for wkey in writes:
            self.lastw[wkey] = o
            self.readers[wkey] = []
        for r in reads:
            self.readers.setdefault(r, []).append(o)
        if dma is not None:
            o.sem = dma
            o.signal = True
        self.ops.append(o)
        return o

    def emit(self, stack):
        nc = self.nc
        for e in COMPUTE:
            self.eng_sems[e] = stack.enter_context(nc.semaphore("s_" + e))
        dkeys = []
        seen = set()
        for o in self.ops:
            if o.is_dma and o.sem not in seen:
                seen.add(o.sem)
                dkeys.append(o.sem)
        dsem = {k: stack.enter_context(nc.semaphore("d%d" % i)) for i, k in enumerate(dkeys)}
        cnt = {e: 0 for e in COMPUTE}
        dcnt = {k: 0 for k in dkeys}
        for o in self.ops:
            if o.is_dma:
                dcnt[o.sem] += 16
                o.count = dcnt[o.sem]
                o.sem = dsem[o.sem]
            elif o.signal:
                cnt[o.eng] += 1
                o.count = cnt[o.eng]
                o.sem = self.eng_sems[o.eng]
        block = stack.enter_context(nc.Block())
        by_eng = {}
        for o in self.ops:
            by_eng.setdefault(o.eng, []).append(o)

        def run(engname, eng):
            waited = {}
            for o in by_eng.get(engname, ()):
                need = {}
                for d in o.deps:
                    k = id(d.sem)
                    if need.get(k, (None, 0))[1] < d.count:
                        need[k] = (d.sem, d.count)
                for k, (sem, c) in need.items():
                    if waited.get(k, 0) >= c:
                        continue
                    eng.wait_ge(sem, c)
                    waited[k] = c
                if o.fn is None:
                    continue
                ins = o.fn(eng)
                if o.is_dma:
                    ins.then_inc(o.sem, 16)
                elif o.signal:
                    ins.then_inc(o.sem, 1)

        @block.tensor
        def _(e):
            run("pe", e)

        @block.scalar
        def _(e):
            run("act", e)

        @block.vector
        def _(e):
            run("dve", e)

        @block.gpsimd
        def _(e):
            run("pool", e)

        @block.sync
        def _(e):
            run("sp", e)


def _blocks(Wm):
    K, C = Wm.shape
    nk, M = K // 128, C // 128
    return np.ascontiguousarray(
        Wm.reshape(nk, 128, M, 128).transpose(2, 1, 0, 3).reshape(M * 128, nk * 128))


def _col(v):
    return np.ascontiguousarray(np.asarray(v, np.float32).reshape(-1, 128).T)


class PPLayout:
    def __init__(self):
        self.off = {}
        self.n = 0

    def add(self, name, n):
        self.off[name] = self.n
        self.n += n


def _pp_layout():
    L = PPLayout()
    for l in range(2):
        for nm in ("mix_pre", "mix_post", "ffn_pre", "ffn_post"):
            L.add("%s%d" % (nm, l), 16)
    L.add("ab_conv", 24)
    L.add("ab_pscale", 8)
    L.add("rden15", 60)
    L.add("b_pw1", 32)
    L.add("wst", 32 * 16)
    L.add("b_dw", 16)
    L.add("ln_g", 16)
    L.add("ln_b", 16)
    L.add("b_pw2", 16)
    for l in range(2):
        L.add("ffn_conv%d" % l, 3 * 88)
    L.add("eps", 1)
    L.add("zero", 1)
    L.n = (L.n + 7) // 8 * 8
    return L


PPL = _pp_layout()


def build_nc(n_sub=4):
    nc = bass.Bass("TRN2", target_bir_lowering=False)
    dr = {}

    def din(name, shape):
        dr[name] = nc.dram_tensor(name, list(shape), F32, kind="ExternalInput").ap()
        return dr[name]

    din("xT", [128, KD * WIN])
    din("pp", [128, PPL.n])
    din("ident", [128, 32])
    din("poolw", [128, 4 * 2 * 256])
    din("w_in", [32 * 128, D])
    din("w_out", [16 * 128, D])
    din("pw1", [32 * 128, D])
    din("pw2", [16 * 128, D])
    for l in range(2):
        din("up%d" % l, [88 * 128, D])
        din("dn%d" % l, [16 * 128, DFF])
    outT = nc.dram_tensor("outT", [128, KD * WIN], F32, kind="ExternalOutput").ap()

    with ExitStack() as st:
        sizes = [
            ("h", KD * WIN * 4), ("pp", PPL.n * 4), ("zT", KD * TH * 2), ("R1", KF * TH * 2),
            ("M", KD * TH * 4), ("ring", NSLOT * 4096), ("rstd", TH * 4), ("mu", TH * 4), ("rstdA", TH * 4), ("sqC", 2 * TH * 2),
            ("identb", 64), ("onesb", 256), ("poolw", 4096),
            ("st_ffn", 2 * KF * 2 * 4), ("st_pa", 8 * 2 * 4), ("st_vb", 8 * 15 * 4), ("st_c", 16 * 30 * 2),
        ]
        offs = {}
        o = 0
        for nm, sz in sizes:
            offs[nm] = o
            o += (sz + 31) // 32 * 32
        total = o
        arena = st.enter_context(nc.sbuf_tensor("arena", [128, total // 4], F32))

        def view(off, nbytes, dt):
            a = arena[:, off // 4:(off + nbytes) // 4]
            return a if dt == F32 else a.bitcast(dt)

        hT = view(offs["h"], KD * WIN * 4, F32).rearrange("p (k t) -> p k t", k=KD)
        ppt = view(offs["pp"], PPL.n * 4, F32)
        zT = view(offs["zT"], KD * TH * 2, BF16).rearrange("p (k t) -> p k t", k=KD)
        yF = view(offs["R1"], KF * TH * 2, BF16).rearrange("p (k t) -> p k t", k=KF)
        cR = view(offs["R1"], KD * TH * 4, F32).rearrange("p (k t) -> p k t", k=KD)
        mH = view(offs["M"], KD * TH * 4, F32).rearrange("p (k t) -> p k t", k=KD)
        ring = [view(offs["ring"] + s * 4096, 4096, BF16).rearrange("p (k c) -> p k c", k=16)
                for s in range(NSLOT)]
        rstd = view(offs["rstd"], TH * 4, F32)
        mu = view(offs["mu"], TH * 4, F32)
        rstdA = view(offs["rstdA"], TH * 4, F32)
        sqC = [view(offs["sqC"] + i * TH * 2, TH * 2, BF16) for i in range(2)]
        sqA = [view(offs["mu"] + i * TH * 2, TH * 2, BF16) for i in range(2)]
        identb = view(offs["identb"], 64, BF16)
        onesb = view(offs["onesb"], 256, BF16)
        poolwb = view(offs["poolw"], 4096, BF16).rearrange("p (g i o) -> p g i o", g=4, i=2)
        st_ffn = view(offs["st_ffn"], 2 * KF * 2 * 4, F32).rearrange("p (a k t) -> p a k t", a=2, k=KF)
        st_pa = view(offs["st_pa"], 8 * 2 * 4, F32).rearrange("p (k t) -> p k t", k=8)
        st_vb = view(offs["st_vb"], 8 * 15 * 4, F32).rearrange("p (k t) -> p k t", k=8)
        st_c = view(offs["st_c"], 16 * 30 * 2, BF16).rearrange("p (k t) -> p k t", k=16)

        pst = [st.enter_context(nc.psum_tensor("ps%d" % i, [128, 2, 512], F32)) for i in range(4)]
        PROJ_T = (0, 1)
        S1, S2 = 2, 3

        def pcol(name, i=0):
            c = PPL.off[name] + i
            return ppt[:, c:c + 1]

        def v2(ap):
            return ap.rearrange("p (n t) -> p n t", n=2)

        def psv(i):
            return pst[i][:, :, 0:NT]

        MCH = TH * 4

        class Scr:
            reg = []

            def __init__(self, off, size):
                self.off, self.size = off, size
                self.key = ("sc", off)
                c0, c1 = off // MCH, (off + size - 1) // MCH
                self.mkeys = [("M", c) for c in range(c0, c1 + 1)]
                if not any(t.off == off and t.size == size for t in Scr.reg):
                    Scr.reg.append(self)

            def f32(self, n, o=0):
                return view(offs["M"] + self.off + o * 4, n * 4, F32)

            def bf(self, n, o=0):
                return view(offs["M"] + self.off + o * 2, n * 2, BF16)

        RTAIL = (34 * TH * 2 + 31) // 32 * 32
        assert RTAIL + 4 * 2304 <= KF * TH * 2

        class RScr:
            reg = []

            def __init__(self, off, size, ylo=34, extra=()):
                self.off, self.size = off, size
                self.key = ("rs", off, size)
                self.extra = list(extra)
                self.mkeys = [("y", k) for k in range(ylo, KF)]
                for t2 in RScr.reg:
                    if t2.off < off + size and t2.off + t2.size > off and t2.key != self.key:
                        self.mkeys += [t2.key] + t2.extra
                        t2.mkeys += [self.key] + self.extra
                if not any(t2.key == self.key for t2 in RScr.reg):
                    RScr.reg.append(self)

            def f32(self, n, o=0):
                return view(offs["R1"] + self.off + o * 4, n * 4, F32)

            def bf(self, n, o=0):
                return view(offs["R1"] + self.off + o * 2, n * 2, BF16)

        def gen_tiles(base, n, size=2304):
            return [Scr(base + i * size, size) for i in range(n)]

        def program(P, dry):
            wplan = program.wplan
            state = {"wi": 0, "ps": 0, "fresh": set(), "pre_done": False, "nxt": None, "tail": None,
                     "ptiles": PROJ_T}

            def issue(bi, extra_reads=()):
                if bi >= len(wplan):
                    return
                src, nk = wplan[bi]
                slot = bi % NSLOT
                P.op("pool", lambda e: e.dma_start(out=ring[slot][:, 0:nk, :], in_=src(),
                                                    max_dma_last_dim=8192),
                     reads=list(extra_reads), writes=[("w", slot)], dma=("w", slot))

            def load_x(half, extra_reads):
                xv = dr["xT"].rearrange("p (k t) -> p k t", k=KD)
                for kk in range(4):
                    P.op("sp", lambda e, half=half, kk=kk: e.dma_start(
                        out=hT[:, kk * 4:(kk + 1) * 4, half * TH:(half + 1) * TH],
                        in_=xv[:, kk * 4:(kk + 1) * 4, half * TH:(half + 1) * TH]),
                        reads=list(extra_reads),
                        writes=[("h", k, half) for k in range(kk * 4, kk * 4 + 4)], dma=("x", half, kk))

            def wget(src, nk):
                bi = state["wi"]
                state["wi"] += 1
                if dry:
                    wplan.append((src, nk))
                    return 0
                if bi == 0:
                    hk0 = [("h", k, 0) for k in range(KD)]
                    for j in range(NSLOT):
                        issue(j, hk0)
                    load_x(1, [("w", sl) for sl in range(NSLOT)])
                else:
                    issue(bi + NSLOT - 1)
                return bi % NSLOT

            def wsrc(name, m, k0, nk):
                return lambda: dr[name][m * 128:(m + 1) * 128, k0 * 128:(k0 + nk) * 128].rearrange(
                    "p (k c) -> p k c", c=128)

            def psnext():
                tl = state["ptiles"]
                i = tl[state["ps"] % len(tl)]
                state["ps"] += 1
                return i

            def drain_tail(chunks):
                tl = state["tail"]
                if tl is None:
                    return
                if chunks == KD:
                    chunks = range(KD)
                for c in sorted(set(chunks) & tl[1]):
                    tl[0](c)
                    tl[1].discard(c)
                if not tl[1]:
                    state["tail"] = None

            def scw(t):
                if t.key in state["fresh"]:
                    return [t.key]
                state["fresh"].add(t.key)
                drain_tail([mk[1] for mk in t.mkeys])
                return [t.key] + t.mkeys

            def proj(parts, rhs, rkey, first_extra=()):
                pt = psnext()
                ktot = sum(p[3] for p in parts)
                kk = 0
                for (name, m, k0, nk) in parts:
                    slot = wget(wsrc(name, m, k0, nk), nk)
                    for k in range(nk):
                        for nt, (o, n) in enumerate(NTS):
                            P.op("pe", (lambda e, slot=slot, k=k, nt=nt, o=o, n=n, kk=kk, pt=pt, kg=k0 + k:
                                        e.matmul(pst[pt][:, nt, 0:n], lhsT=ring[slot][:, k, :],
                                                 rhs=rhs(kg)[:, o:o + n],
                                                 start=(kk == 0), stop=(kk == ktot - 1))),
                                 reads=[("w", slot), rkey(k0 + k)] + (list(first_extra) if kk == 0 else []),
                                 writes=[("ps", pt)])
                        kk += 1
                return pt

            def ones_mm(tile, src_ap, skey, first, last):
                for nt, (o, n) in enumerate(NTS):
                    P.op("pe", (lambda e, nt=nt, o=o, n=n:
                                e.matmul(pst[tile][:, nt, 0:n], lhsT=onesb, rhs=src_ap[:, o:o + n],
                                         start=first, stop=last)),
                         reads=[skey, "onesb"], writes=[("ps", tile)])

            def rstd_from(tile, dst, dkey):
                P.op("act", lambda e: e.activation(out=v2(dst), in_=psv(tile), func=AF.Ln,
                                                   scale=1.0 / D, bias=pcol("eps")),
                     reads=[("ps", tile), "pp"], writes=[dkey])
                P.op("act", lambda e: e.activation(out=dst, in_=dst, func=AF.Exp, scale=-0.5),
                     reads=[dkey], writes=[dkey])

            def A_sq(k, half):
                t0 = half * TH
                i = k % 2
                wk = [("sqA", i)] + (["mu"] if k < 2 else [])
                P.op("act", lambda e, k=k, i=i: e.activation(out=sqA[i], in_=hT[:, k, t0:t0 + TH],
                                                              func=AF.Square),
                     reads=[("h", k, half)], writes=wk)

            def A_add(p):
                P.op("dve", lambda e: e.tensor_tensor(out=sqA[0], in0=sqA[0], in1=sqA[1], op=ALU.add),
                     reads=[("sqA", 0), ("sqA", 1)], writes=[("sqA", 0)])

            def A_pmm(p):
                ones_mm(S2, sqA[0], ("sqA", 0), p == 0, p == KD // 2 - 1)

            def A_pair(p):
                A_add(p)
                A_pmm(p)

            def A_fin():
                rstd_from(S2, rstdA, "rstdA")

            def A_z(k, half, gname):
                t0 = half * TH
                P.op("dve", lambda e, k=k: e.scalar_tensor_tensor(
                    out=zT[:, k, :], in0=hT[:, k, t0:t0 + TH], scalar=pcol(gname, k), in1=rstdA,
                    op0=ALU.mult, op1=ALU.mult),
                    reads=[("h", k, half), "rstdA", "pp"], writes=[("zT", k)])

            def phaseA(half, gname):
                if state["pre_done"]:
                    state["pre_done"] = False
                    return
                t0_ = half * TH
                for k in range(KD):
                    if k % 2 == 0:
                        A_sq(k, half)
                    else:
                        P.op("dve", lambda e, k=k: e.tensor_tensor(
                            out=sqA[1], in0=hT[:, k, t0_:t0_ + TH], in1=hT[:, k, t0_:t0_ + TH], op=ALU.mult),
                            reads=[("h", k, half)], writes=[("sqA", 1)] + (["mu"] if k < 2 else []))
                    ones_mm(S2, sqA[k % 2], ("sqA", k % 2), k == 0, k == KD - 1)
                A_fin()
                for k in range(KD):
                    A_z(k, half, gname)

            def phaseC(half, wname, nkc, rhs, rkey, gname, bias_name, sq_tiles, overlap=True, first_extra=()):
                t0 = half * TH
                parts_of = lambda m: [(wname, m, k0, min(16, nkc - k0)) for k0 in range(0, nkc, 16)]
                pend = None
                state["ptiles"] = PROJ_T
                nxt = state["nxt"] if overlap else None
                for m in range(KD):
                    if m == 0:
                        drain_tail(KD)
                    if nxt is not None:
                        nh, ng = nxt
                        if 1 <= m <= 8:
                            A_pmm(m - 1)
                        if m < 8:
                            A_sq(2 * m, nh)
                            A_sq(2 * m + 1, nh)
                            A_add(m)
                        else:
                            if m == 8:
                                A_fin()
                            A_z(2 * (m - 8), nh, ng)
                            A_z(2 * (m - 8) + 1, nh, ng)
                    pt = proj(parts_of(m), rhs, rkey, first_extra if m == 0 else ())
                    if pend is not None:
                        ones_mm(S1, pend[0], pend[1], pend[2] == 0, False)
                        pend = None
                    b = pcol(bias_name, m) if bias_name else 0.0
                    alias = [t.key for t in Scr.reg
                             if t.off < (m + 1) * MCH and t.off + t.size > m * MCH]
                    P.op("act", lambda e, m=m, pt=pt, b=b: e.activation(
                        out=v2(mH[:, m, :]), in_=psv(pt), func=AF.Identity, bias=b),
                        reads=[("ps", pt), "pp"], writes=[("M", m)] + alias)
                    sq_ap, sq_key = sq_tiles[m % 2]
                    P.op("act", lambda e, pt=pt, b=b, sq_ap=sq_ap: e.activation(
                        out=v2(sq_ap), in_=psv(pt), func=AF.Square, bias=b),
                        reads=[("ps", pt), "pp"], writes=[sq_key])
                    if m % 2 == 1 and m < KD - 1:
                        (s0, k0_), (s1, k1_) = sq_tiles[0], sq_tiles[1]
                        P.op("dve", lambda e, s0=s0, s1=s1: e.tensor_tensor(out=s0, in0=s0, in1=s1, op=ALU.add),
                             reads=[k0_, k1_], writes=[k0_])
                        pend = (s0, k0_, m // 2)
                    elif m == KD - 2:
                        pend = (sq_ap, sq_key, m // 2)
                    elif m == KD - 1:
                        pend = (sq_ap, sq_key, m // 2)
                ones_mm(S1, pend[0], pend[1], False, True)
                if nxt is not None:
                    state["pre_done"] = True
                rstd_from(S1, rstd, "rstd")

                def tail(m):
                    P.op("dve", lambda e, m=m: e.scalar_tensor_tensor(
                        out=mH[:, m, :], in0=mH[:, m, :], scalar=pcol(gname, m), in1=rstd,
                        op0=ALU.mult, op1=ALU.mult),
                        reads=[("M", m), "rstd", "pp"], writes=[("M", m)])
                    P.op("dve", lambda e, m=m: e.tensor_tensor(
                        out=hT[:, m, t0:t0 + TH], in0=hT[:, m, t0:t0 + TH], in1=mH[:, m, :], op=ALU.add),
                        reads=[("M", m), ("h", m, half)], writes=[("h", m, half)])
                state["tail"] = [tail, set(range(KD))]

            zsq = [(sqC[i], ("sqC", i)) for i in range(2)]

            def ffn(l, half):
                state["fresh"] = set()
                phaseA(half, "ffn_pre%d" % l)
                state["ptiles"] = (0, 1, S2)
                tilesB = gen_tiles(0, 4)
                tilesAM = gen_tiles(4 * 2304, 4)
                tilesAR = [RScr(RTAIL + j * 2304, 2304) for j in range(4)]
                up = "up%d" % l
                cw = "ffn_conv%d" % l
                for i in range(KF):
                    r = i % 2
                    ug, tg, uv, tv = tilesB if r == 1 else (tilesAR if i < 30 else tilesAM)
                    if 8 <= i < 20:
                        drain_tail([i - 4])
                    res = []
                    for which, (u_t, t_t, blk) in enumerate(((ug, tg, i), (uv, tv, KF + i))):
                        pt = proj([(up, blk, 0, 16)], lambda k: zT[:, k, :], lambda k: ("zT", k))
                        u = u_t.f32(TH + 2)
                        t = t_t.f32(TH)
                        if half == 0:
                            P.op("pool", lambda e, u=u: e.memset(u[:, 0:2], 0.0), writes=scw(u_t))
                        else:
                            P.op("pool", lambda e, u=u, which=which, i=i: e.tensor_copy(
                                out=u[:, 0:2], in_=st_ffn[:, which, i, :]),
                                reads=[("st_ffn", which, i)], writes=scw(u_t))
                        P.op("act", lambda e, u=u, pt=pt: e.activation(
                            out=v2(u[:, 2:TH + 2]), in_=psv(pt), func=AF.Copy),
                            reads=[("ps", pt)], writes=[u_t.key])
                        P.op("act", lambda e, t=t, pt=pt, blk=blk: e.activation(
                            out=v2(t), in_=psv(pt), func=AF.Copy, scale=pcol(cw, 2 * 88 + blk)),
                            reads=[("ps", pt), "pp"], writes=scw(t_t))
                        if half == 0:
                            P.op("pool", lambda e, u=u, which=which, i=i: e.tensor_copy(
                                out=st_ffn[:, which, i, :], in_=u[:, TH:TH + 2]),
                                reads=[u_t.key], writes=[("st_ffn", which, i)])
                        P.op("dve", lambda e, u=u, t=t, blk=blk: e.scalar_tensor_tensor(
                            out=t, in0=u[:, 1:TH + 1], scalar=pcol(cw, 88 + blk), in1=t,
                            op0=ALU.mult, op1=ALU.add),
                            reads=[u_t.key, t_t.key, "pp"], writes=[t_t.key])
                        P.op("dve", lambda e, u=u, t=t, blk=blk: e.scalar_tensor_tensor(
                            out=t, in0=u[:, 0:TH], scalar=pcol(cw, blk), in1=t,
                            op0=ALU.mult, op1=ALU.add),
                            reads=[u_t.key, t_t.key, "pp"], writes=[t_t.key])
                        res.append(t)
                    P.op("act", lambda e, tgv=res[0]: e.activation(out=tgv, in_=tgv, func=AF.Silu),
                         reads=[tg.key], writes=[tg.key])
                    P.op("dve", lambda e, tgv=res[0], tvv=res[1], i=i: e.tensor_tensor(
                        out=yF[:, i, :], in0=tgv, in1=tvv, op=ALU.mult),
                        reads=[tg.key, tv.key],
                        writes=[("y", i)] + [kk_ for t in RScr.reg
                                             if t.off < (i + 1) * TH * 2 and t.off + t.size > i * TH * 2
                                             for kk_ in [t.key] + t.extra])
                phaseC(half, "dn%d" % l, KF, lambda k: yF[:, k, :], lambda k: ("y", k),
                       "ffn_post%d" % l, None, zsq)

            def mixer0(half):
                state["fresh"] = set()
                phaseA(half, "mix_pre0")
                state["ptiles"] = (0, 1, S2)
                RM0 = (KD * TH * 2 + 31) // 32 * 32
                assert RM0 + 12 * 2304 <= KF * TH * 2
                tiles = [RScr(RM0 + j_ * 2304, 2304, KD) for j_ in range(12)]
                zr = lambda k: zT[:, k, :]
                zk = lambda k: ("zT", k)
                for j in range(8):
                    r = j % 2
                    drain_tail([j])
                    gc_t, pa_t, t_t = tiles[r * 3:(r + 1) * 3]
                    gc = gc_t.f32(TH)
                    pa = pa_t.f32(TH + 2)
                    t = t_t.f32(TH)
                    p_gc = proj([("w_in", 8 + j, 0, 16)], zr, zk)
                    P.op("act", lambda e, gc=gc, p=p_gc: e.activation(out=v2(gc), in_=psv(p), func=AF.Copy),
                         reads=[("ps", p_gc)], writes=scw(gc_t))
                    p_va = proj([("w_in", 16 + j, 0, 16)], zr, zk)
                    if half == 0:
                        P.op("pool", lambda e, pa=pa: e.memset(pa[:, 0:2], 0.0), writes=scw(pa_t))
                    else:
                        P.op("pool", lambda e, pa=pa, j=j: e.tensor_copy(out=pa[:, 0:2], in_=st_pa[:, j, :]),
                             reads=[("st_pa", j)], writes=scw(pa_t))
                    P.op("dve", lambda e, pa=pa, gc=gc, p=p_va: e.tensor_tensor(
                        out=v2(pa[:, 2:TH + 2]), in0=v2(gc), in1=psv(p), op=ALU.mult),
                        reads=[gc_t.key, ("ps", p_va)], writes=[pa_t.key])
                    if half == 0:
                        P.op("pool", lambda e, pa=pa, j=j: e.tensor_copy(out=st_pa[:, j, :], in_=pa[:, TH:TH + 2]),
                             reads=[pa_t.key], writes=[("st_pa", j)])
                    P.op("dve", lambda e, pa=pa, t=t, j=j: e.tensor_scalar(
                        out=t, in0=pa[:, 2:TH + 2], scalar1=pcol("ab_conv", 16 + j), scalar2=None, op0=ALU.mult),
                        reads=[pa_t.key, "pp"], writes=scw(t_t))
                    P.op("dve", lambda e, pa=pa, t=t, j=j: e.scalar_tensor_tensor(
                        out=t, in0=pa[:, 1:TH + 1], scalar=pcol("ab_conv", 8 + j), in1=t,
                        op0=ALU.mult, op1=ALU.add), reads=[pa_t.key, t_t.key, "pp"], writes=[t_t.key])
                    P.op("dve", lambda e, pa=pa, t=t, j=j: e.scalar_tensor_tensor(
                        out=t, in0=pa[:, 0:TH], scalar=pcol("ab_conv", j), in1=t,
                        op0=ALU.mult, op1=ALU.add), reads=[pa_t.key, t_t.key, "pp"], writes=[t_t.key])
                    p_gb = proj([("w_in", j, 0, 16)], zr, zk)
                    P.op("dve", lambda e, t=t, p=p_gb, j=j: e.tensor_tensor(
                        out=v2(yF[:, j, :]), in0=v2(t), in1=psv(p), op=ALU.mult),
                        reads=[t_t.key, ("ps", p_gb)], writes=[("y", j)])
                HL = 15
                EL = TH + HL
                pgbase = 12 * 2304
                pgt = [[Scr(pgbase + (2 * s_ + ic) * 1088, 1088) for ic in range(2)] for s_ in range(2)]
                tmp_t = Scr(pgbase + 4 * 1088, 64)

                def pool_mm(g):
                    pgs = [pgt[g % 2][ic].bf(TH) for ic in range(2)]
                    for oc in range(2):
                        pt = psnext()
                        for ic in range(2):
                            for nt, (o, n) in enumerate(NTS):
                                P.op("pe", (lambda e, pt=pt, nt=nt, o=o, n=n, ic=ic, oc=oc, g=g, pgs=pgs:
                                            e.matmul(pst[pt][:, nt, 0:n],
                                                     lhsT=poolwb[:, g, ic, oc * 128:(oc + 1) * 128],
                                                     rhs=pgs[ic][:, o:o + n], start=(ic == 0), stop=(ic == 1))),
                                     reads=[pgt[g % 2][ic].key, "poolw"], writes=[("ps", pt)])
                        P.op("act", lambda e, pt=pt, g=g, oc=oc: e.activation(
                            out=v2(yF[:, 8 + 2 * g + oc, :]), in_=psv(pt), func=AF.Copy,
                            scale=pcol("ab_pscale", 2 * g + oc)),
                            reads=[("ps", pt), "pp"], writes=[("y", 8 + 2 * g + oc)])
                for c in range(8):
                    g = c // 2
                    drain_tail([8 + c])
                    wdw = 2 << g
                    r = c % 2
                    e_t, sa_t, sb_t = tiles[6 + r * 3:6 + (r + 1) * 3]
                    pg_t = pgt[g % 2][c % 2]
                    ee = e_t.f32(EL)
                    sa = sa_t.f32(EL)
                    sb = sb_t.f32(EL)
                    p_vb = proj([("w_in", 24 + c, 0, 16)], zr, zk)
                    if half == 0:
                        P.op("pool", lambda e, ee=ee: e.memset(ee[:, 0:HL], 0.0), writes=scw(e_t))
                    else:
                        P.op("pool", lambda e, ee=ee, c=c: e.tensor_copy(out=ee[:, 0:HL], in_=st_vb[:, c, :]),
                             reads=[("st_vb", c)], writes=scw(e_t))
                    P.op("act", lambda e, ee=ee, p=p_vb: e.activation(
                        out=v2(ee[:, HL:EL]), in_=psv(p), func=AF.Copy),
                        reads=[("ps", p_vb)], writes=[e_t.key])
                    if half == 0:
                        P.op("pool", lambda e, ee=ee, c=c: e.tensor_copy(out=st_vb[:, c, :], in_=ee[:, TH:EL]),
                             reads=[e_t.key], writes=[("st_vb", c)])
                    if c % 2 == 1 and g >= 1:
                        pool_mm(g - 1)
                    src, src_t = ee, e_t
                    sh = 1
                    lo = 0
                    bufs = [(sa, sa_t), (sb, sb_t)]
                    bi = 0
                    while sh < wdw:
                        dst, dst_t = bufs[bi]
                        bi ^= 1
                        lo2 = lo + sh
                        P.op("dve", lambda e, dst=dst, src=src, lo2=lo2, sh=sh: e.tensor_tensor(
                            out=dst[:, lo2:EL], in0=src[:, lo2:EL], in1=src[:, lo2 - sh:EL - sh], op=ALU.add),
                            reads=[src_t.key], writes=scw(dst_t))
                        src, src_t = dst, dst_t
                        lo = lo2
                        sh *= 2
                    pgb = pg_t.bf(TH)
                    P.op("dve", lambda e, pgb=pgb, src=src, ee=ee, wdw=wdw: e.scalar_tensor_tensor(
                        out=pgb, in0=src[:, HL:EL], scalar=1.0 / wdw, in1=ee[:, HL:EL],
                        op0=ALU.mult, op1=ALU.subtract),
                        reads=[src_t.key, e_t.key], writes=scw(pg_t))
                    if half == 0:
                        tmp = tmp_t.f32(HL)
                        P.op("dve", lambda e, tmp=tmp, src=src, g=g: e.tensor_tensor(
                            out=tmp, in0=src[:, HL:2 * HL],
                            in1=ppt[:, PPL.off["rden15"] + g * 15:PPL.off["rden15"] + (g + 1) * 15], op=ALU.mult),
                            reads=[src_t.key, "pp"], writes=scw(tmp_t))
                        P.op("dve", lambda e, tmp=tmp, pgb=pgb, ee=ee: e.tensor_tensor(
                            out=pgb[:, 0:HL], in0=tmp, in1=ee[:, HL:2 * HL], op=ALU.subtract),
                            reads=[tmp_t.key, pg_t.key, e_t.key], writes=[pg_t.key])
                pool_mm(3)
                phaseC(half, "w_out", 16, lambda k: yF[:, k, :], lambda k: ("y", k), "mix_post0", None, zsq)

            def mixer1(half):
                state["fresh"] = set()
                phaseA(half, "mix_pre1")
                state["ptiles"] = PROJ_T
                yC = lambda k: cR[:, k, 0:NT].bitcast(BF16)
                XL = TH + 28
                LB = 32 * 32 * 2
                XB = 4 * XL * 2
                tiles = gen_tiles(0, 8)
                TB = 8 * 2304
                RB = KD * TH * 4
                assert RB + 2 * XB + LB <= KF * TH * 2
                Xt = [RScr(RB + r_ * XB, XB, 32, [("X", r_, g, j) for g in range(4) for j in range(4)])
                      for r_ in range(2)]
                Lt = [RScr(RB + 2 * XB, LB, 32), Scr(TB, LB)]
                zr = lambda k: zT[:, k, :]
                zk = lambda k: ("zT", k)
                CL = TH + 30
                pend1 = None
                pend_conv = None
                for i in range(KD):
                    r = i % 2
                    sg_t, cin_t, cs_t = tiles[r * 3:(r + 1) * 3]
                    sg = sg_t.f32(TH)
                    cin = cin_t.bf(CL)
                    cb = cs_t.bf(TH)
                    sqb = cs_t.bf(TH, o=TH)
                    L_t, X_t = Lt[r], Xt[r]
                    Lb = L_t.bf(32 * 32).rearrange("p (a c) -> p a c", a=32)
                    Xb = X_t.bf(4 * XL).rearrange("p (g t) -> p g t", g=4)
                    xkeys = [("X", r, g, j) for g in range(4) for j in range(4)]
                    p_g = proj([("pw1", 16 + i, 0, 16)], zr, zk)
                    P.op("act", lambda e, sg=sg, p=p_g, i=i: e.activation(
                        out=v2(sg), in_=psv(p), func=AF.Sigmoid, bias=pcol("b_pw1", 16 + i)),
                        reads=[("ps", p_g), "pp"], writes=scw(sg_t))
                    p_a = proj([("pw1", i, 0, 16)], zr, zk)
                    if half == 0:
                        P.op("pool", lambda e, cin=cin: e.memset(cin[:, 0:30], 0.0), writes=scw(cin_t))
                    else:
                        P.op("pool", lambda e, cin=cin, i=i: e.tensor_copy(out=cin[:, 0:30], in_=st_c[:, i, :]),
                             reads=[("st_c", i)], writes=scw(cin_t))
                    P.op("dve", lambda e, cin=cin, sg=sg, p=p_a, i=i: e.scalar_tensor_tensor(
                        out=v2(cin[:, 30:CL]), in0=psv(p), scalar=pcol("b_pw1", i), in1=v2(sg),
                        op0=ALU.add, op1=ALU.mult),
                        reads=[("ps", p_a), sg_t.key, "pp"], writes=[cin_t.key])
                    if half == 0:
                        P.op("pool", lambda e, cin=cin, i=i: e.tensor_copy(out=st_c[:, i, :], in_=cin[:, TH:CL]),
                             reads=[cin_t.key], writes=[("st_c", i)])
                    wd = ppt[:, PPL.off["wst"] + i * 32:PPL.off["wst"] + (i + 1) * 32]
                    P.op("dve", lambda e, Lb=Lb, wd=wd: e.tensor_tensor(
                        out=Lb, in0=identb[:, None, :].broadcast_to([128, 32, 32]),
                        in1=wd[:, :, None].broadcast_to([128, 32, 32]), op=ALU.mult),
                        reads=["identb", "pp"], writes=scw(L_t))
                    P.op("pool", lambda e, Xb=Xb: e.memset(Xb[:, :, XL - 1:XL], 0.0),
                         writes=scw(X_t) + xkeys)
                    for g in range(4):
                        q = "sp" if g < 2 else "pool"
                        for j in range(4):
                            ncol = XL if j < 3 else XL - 1
                            P.op(q, lambda e, g=g, j=j, ncol=ncol, Xb=Xb, cin=cin: e.dma_start(
                                out=Xb[32 * j:32 * j + 32, g, 0:ncol], in_=cin[32 * g:32 * g + 32, j:j + ncol]),
                                reads=[cin_t.key], writes=[("X", r, g, j)], dma=("xs", r, q))

                    def conv_evac(i=i, r=r, Lb=Lb, Xb=Xb, L_t=L_t, xkeys=xkeys, cs_t=cs_t, cb=cb, sqb=sqb):
                        pt = psnext()
                        for b_ in range(8):
                            for nt, (o, n) in enumerate(NTS):
                                for g in range(4):
                                    P.op("pe", (lambda e, pt=pt, nt=nt, o=o, n=n, b_=b_, g=g:
                                                e.matmul(pst[pt][32 * g:32 * g + 32, nt, 0:n],
                                                         lhsT=Lb[:, g * 8 + b_, :],
                                                         rhs=Xb[:, g, o + 4 * b_:o + 4 * b_ + n],
                                                         start=(b_ == 0), stop=(b_ == 7),
                                                         tile_position=(0, 32 * g))),
                                         reads=[L_t.key] + xkeys, writes=[("ps", pt)])
                        P.op("act", lambda e, pt=pt: e.activation(
                            out=v2(cR[:, i, :]), in_=psv(pt), func=AF.Identity, bias=pcol("b_dw", i)),
                            reads=[("ps", pt), "pp"], writes=[("c", i)] + ([("csq", i)] if i < 4 else []))
                        P.op("act", lambda e, pt=pt: e.activation(
                            out=v2(cb), in_=psv(pt), func=AF.Identity, bias=pcol("b_dw", i)),
                            reads=[("ps", pt), "pp"], writes=scw(cs_t))
                        P.op("act", lambda e, pt=pt: e.activation(
                            out=v2(sqb), in_=psv(pt), func=AF.Square, bias=pcol("b_dw", i)),
                            reads=[("ps", pt), "pp"], writes=[cs_t.key])
                        return (cb, sqb, cs_t.key, i)

                    if pend_conv is not None:
                        if pend1 is not None:
                            ones_mm(S1, pend1[0], pend1[2], pend1[3] == 0, False)
                            ones_mm(S2, pend1[1], pend1[2], pend1[3] == 0, False)
                        pend1 = pend_conv()
                    pend_conv = conv_evac
                ones_mm(S1, pend1[0], pend1[2], False, False)
                ones_mm(S2, pend1[1], pend1[2], False, False)
                pend1 = pend_conv()
                ones_mm(S1, pend1[0], pend1[2], False, True)
                ones_mm(S2, pend1[1], pend1[2], False, True)
                drain_tail(KD)
                t6, t7 = tiles[6], tiles[7]
                ex2 = t6.f32(TH)
                msq = t7.f32(TH)
                P.op("act", lambda e: e.activation(out=v2(msq), in_=psv(S1), func=AF.Square, scale=1.0 / D),
                     reads=[("ps", S1)], writes=scw(t7))
                P.op("act", lambda e: e.activation(out=v2(mu), in_=psv(S1), func=AF.Copy, scale=1.0 / D),
                     reads=[("ps", S1)], writes=["mu", ("sqA", 0), ("sqA", 1)])
                P.op("dve", lambda e: e.scalar_tensor_tensor(out=v2(ex2), in0=psv(S2), scalar=1.0 / D, in1=v2(msq),
                                                             op0=ALU.mult, op1=ALU.subtract),
                     reads=[("ps", S2), t7.key], writes=scw(t6))
                P.op("act", lambda e: e.activation(out=rstd, in_=ex2, func=AF.Ln, scale=1.0, bias=pcol("eps")),
                     reads=[t6.key, "pp"], writes=["rstd"])
                P.op("act", lambda e: e.activation(out=rstd, in_=rstd, func=AF.Exp, scale=-0.5),
                     reads=["rstd"], writes=["rstd"])
                for i in range(KD):
                    lt_t = tiles[i % 2 * 3]
                    lt = lt_t.f32(TH)
                    P.op("dve", lambda e, lt=lt, i=i: e.tensor_tensor(out=lt, in0=cR[:, i, :], in1=mu, op=ALU.subtract),
                         reads=[("c", i), "mu"], writes=scw(lt_t))
                    P.op("dve", lambda e, lt=lt: e.tensor_tensor(out=lt, in0=lt, in1=rstd, op=ALU.mult),
                         reads=[lt_t.key, "rstd"], writes=[lt_t.key])
                    P.op("act", lambda e, lt=lt, i=i: e.activation(
                        out=yC(i), in_=lt, func=AF.Silu, scale=pcol("ln_g", i), bias=pcol("ln_b", i)),
                        reads=[lt_t.key, "pp"], writes=[("c", i)])
                csq = [(cR[:, j, NT:TH].bitcast(BF16), ("csq", j)) for j in range(4)]
                phaseC(half, "pw2", 16, yC, lambda k: ("c", k), "mix_post1", "b_pw2", csq,
                       first_extra=[("c", KD - 8)])

            P.op("sp", lambda e: e.dma_start(out=ppt, in_=dr["pp"]), writes=["pp"], dma="ld_pp")
            P.op("pool", lambda e: e.dma_start(out=identb, in_=dr["ident"]), writes=["identb"], dma="ld_id")
            P.op("pool", lambda e: e.dma_start(out=poolwb, in_=dr["poolw"].rearrange(
                "p (g i o) -> p g i o", g=4, i=2)), writes=["poolw"], dma="ld_pw")
            P.op("dve", lambda e: e.memset(onesb, 1.0), writes=["onesb"])
            xv = dr["xT"].rearrange("p (k t) -> p k t", k=KD)
            ov = outT.rearrange("p (k t) -> p k t", k=KD)
            load_x(0, ())
            subs = [lambda h: mixer0(h), lambda h: ffn(0, h), lambda h: mixer1(h), lambda h: ffn(1, h)]
            pre_g = ["mix_pre0", "ffn_pre0", "mix_pre1", "ffn_pre1"]
            steps = [(si, half) for si in range(n_sub) for half in range(2)]
            state["pre_done"] = False
            for n, (si, half) in enumerate(steps):
                if n + 1 < len(steps):
                    state["nxt"] = (steps[n + 1][1], pre_g[steps[n + 1][0]])
                else:
                    state["nxt"] = None
                subs[si](half)
            drain_tail(KD)
            okeys = []
            for half in range(2):
                for kk in range(4):
                    P.op("sp", lambda e, half=half, kk=kk: e.dma_start(
                        out=ov[:, kk * 4:(kk + 1) * 4, half * TH:(half + 1) * TH],
                        in_=hT[:, kk * 4:(kk + 1) * 4, half * TH:(half + 1) * TH]),
                        reads=[("h", k, half) for k in range(kk * 4, kk * 4 + 4)],
                        writes=[("out", half, kk)], dma=("o", half, kk))
                    okeys.append(("out", half, kk))
            P.op("sp", None, reads=okeys)

        program.wplan = []
        program(Prog(nc, dry=True), True)
        P = Prog(nc)
        program(P, False)
        P.emit(st)
    return nc


def _prep_shared(inp):
    f = lambda a: np.asarray(a, np.float32)
    sh = {}
    sh["w_in"] = _blocks(f(inp["ab_w_in"])[0])
    sh["w_out"] = _blocks(f(inp["ab_w_out"])[0])
    sh["pw1"] = _blocks(f(inp["c_w_pw1"])[0])
    sh["pw2"] = _blocks(f(inp["c_w_pw2"])[0])
    for l in range(2):
        sh["up%d" % l] = _blocks(f(inp["ffn_w_up"])[l])
        sh["dn%d" % l] = _blocks(f(inp["ffn_w_down"])[l])
    pw = f(inp["ab_pool_w"])[0]
    sh["poolw"] = np.ascontiguousarray(
        pw.reshape(4, 2, 128, 256).transpose(2, 0, 1, 3).reshape(128, 4 * 2 * 256))
    sh["ident"] = np.ascontiguousarray(np.tile(np.eye(32, dtype=np.float32), (4, 1)))
    pp = np.zeros((128, PPL.n), np.float32)

    def put(name, arr):
        pp[:, PPL.off[name]:PPL.off[name] + arr.shape[1]] = arr

    for l in range(2):
        put("mix_pre%d" % l, _col(f(inp["mix_pre_g"])[l]))
        put("mix_post%d" % l, _col(f(inp["mix_post_g"])[l]))
        put("ffn_pre%d" % l, _col(f(inp["ffn_pre_g"])[l]))
        put("ffn_post%d" % l, _col(f(inp["ffn_post_g"])[l]))
        cw = f(inp["ffn_conv_w"])[l]
        put("ffn_conv%d" % l, np.concatenate([_col(cw[t]) for t in range(3)], axis=1))
    acw = f(inp["ab_conv_w"])[0]
    put("ab_conv", np.concatenate([_col(acw[t]) for t in range(3)], axis=1))
    put("ab_pscale", _col(f(inp["ab_pool_scale"])[0]))
    rd = np.zeros((128, 60), np.float32)
    for g, w in enumerate((2, 4, 8, 16)):
        for p in range(15):
            rd[:, g * 15 + p] = np.float32(1.0) / np.float32(min(p + 1, w))
    put("rden15", rd)
    put("b_pw1", _col(f(inp["c_b_pw1"])[0]))
    wdw = f(inp["c_w_dw"])[0]
    w32 = np.concatenate([wdw, np.zeros((1, D), np.float32)], axis=0)
    wst = w32.reshape(8, 4, 16, 4, 32).transpose(1, 4, 2, 3, 0).reshape(128, 16 * 32)
    put("wst", np.ascontiguousarray(wst))
    put("b_dw", _col(f(inp["c_b_dw"])[0]))
    put("ln_g", _col(f(inp["c_ln_g"])[0]))
    put("ln_b", _col(f(inp["c_ln_b"])[0]))
    put("b_pw2", _col(f(inp["c_b_pw2"])[0]))
    pp[:, PPL.off["eps"]] = EPS
    sh["pp"] = pp
    return sh


def _windows(x, meta):
    B = x.shape[0]
    wins, offs = [], []
    for b in range(B):
        seq = np.concatenate([meta, x[b]], axis=0)
        for c in range(4):
            if c == 0:
                s, ooff = 0, NMETA
            else:
                s, ooff = NMETA + CH * c - HALO, HALO
            w = seq[s:s + WIN]
            wt = np.ascontiguousarray(w.T.reshape(KD, 128, WIN).transpose(1, 0, 2).reshape(128, KD * WIN))
            wins.append(wt)
            offs.append(ooff)
    return wins, offs


_NC_CACHE = {}


def kernel(**inputs):
    x = np.asarray(inputs["x"], np.float32)
    meta = np.asarray(inputs["meta_tokens"], np.float32)
    sh = _prep_shared(inputs)
    wins, offs = _windows(x, meta)
    if "nc" not in _NC_CACHE:
        _NC_CACHE["nc"] = build_nc()
    nc = _NC_CACHE["nc"]
    in_maps = []
    for c in range(8):
        m = dict(sh)
        m["xT"] = wins[c]
        in_maps.append(m)
    res = run_bass_kernel_spmd(nc, in_maps, core_ids=list(range(8)))
    out = np.empty((x.shape[0], SEQ, D), np.float32)
    for c in range(8):
        oT = res.results[c]["outT"].reshape(128, KD, WIN)
        tok = oT.transpose(2, 1, 0).reshape(WIN, D)
        b, q = divmod(c, 4)
        out[b, q * CH:(q + 1) * CH] = tok[offs[c]:offs[c] + CH]
    return out
```
